# Optimizing a Trainium2 kernel written in Bass

```python
import math
import jax, jax.numpy as jnp
from jax import lax
import numpy as np

D_MODEL = 1024
BATCH = 2
SEQ = 8192
DEPTH = 1

CHUNK = 64
D_PLE = 256
D_FF = 2816
RET_HEADS = 8
RET_DK = 128
RET_DV = 256
RET_QK = RET_HEADS * RET_DK
RET_V = RET_HEADS * RET_DV
MLA_HEADS = 8
MLA_NOPE = 128
MLA_ROPE = 64
MLA_DV = 128
Q_LORA = 256
KV_LORA = 256
Q_BLOCK = 128
ROPE_BASE = 10000.0
EPS = 1e-5
N_LN = 4
DEEPNORM_ALPHA = (2.0 * DEPTH) ** 0.25
DEEPNORM_BETA = (8.0 * DEPTH) ** -0.25
SPLITS = (RET_QK, RET_QK, RET_V, RET_V, Q_LORA, KV_LORA, MLA_ROPE, D_MODEL, D_MODEL)
D_IN_TOTAL = sum(SPLITS)

kernel_name = "hybrid_retention_mla_macaron_deepnorm"


def layer_norm(x, g, b):
    xf = x.astype(jnp.float32)
    mu = jnp.mean(xf, axis=-1, keepdims=True)
    var = jnp.mean(jnp.square(xf - mu), axis=-1, keepdims=True)
    y = (xf - mu) * lax.rsqrt(var + EPS)
    return (y * g.astype(jnp.float32) + b.astype(jnp.float32)).astype(x.dtype)


def rms_norm(x, g):
    xf = x.astype(jnp.float32)
    y = xf * lax.rsqrt(jnp.mean(jnp.square(xf), axis=-1, keepdims=True) + EPS)
    return (y * g.astype(jnp.float32)).astype(x.dtype)


def rope(t, positions):
    half = t.shape[-1] // 2
    inv_freq = ROPE_BASE ** (-jnp.arange(half, dtype=jnp.float32) / half)
    ang = positions.astype(jnp.float32)[:, :, None] * inv_freq
    cos = jnp.cos(ang)[:, :, None, :]
    sin = jnp.sin(ang)[:, :, None, :]
    t1 = t[..., :half].astype(jnp.float32)
    t2 = t[..., half:].astype(jnp.float32)
    return jnp.concatenate([t1 * cos - t2 * sin, t2 * cos + t1 * sin], axis=-1).astype(t.dtype)


def swiglu_ffn(x, w_in, w_out):
    g, u = jnp.split(x @ w_in, 2, axis=-1)
    return (jax.nn.silu(g) * u) @ w_out


def chunk_retention(q, k, v):
    B, S, H, dk = q.shape
    dv = v.shape[-1]
    n_chunks = S // CHUNK
    log_gamma = jnp.log(1.0 - 2.0 ** (-5.0 - jnp.arange(H, dtype=jnp.float32)))
    idx = jnp.arange(CHUNK, dtype=jnp.float32)
    intra_decay = jnp.exp(log_gamma[:, None, None] * jnp.abs(idx[:, None] - idx[None, :]))
    xi = jnp.exp(log_gamma[:, None] * (idx + 1.0))
    zeta = jnp.exp(log_gamma[:, None] * (CHUNK - 1.0 - idx))
    chunk_decay = jnp.exp(log_gamma * CHUNK)

    qc = q.reshape(B, n_chunks, CHUNK, H, dk)
    kc = k.reshape(B, n_chunks, CHUNK, H, dk)
    vc = v.reshape(B, n_chunks, CHUNK, H, dv)

    scores = jnp.einsum('bnchd,bnshd->bnhcs', qc, kc) * intra_decay
    y_intra = jnp.einsum('bnhcs,bnshe->bnche', scores, vc)

    xi_ch = xi.T[None, :, :, None]

    def step(state, inp):
        q_n, k_n, v_n = inp
        cross = jnp.einsum('bchd,bhde->bche', q_n, state) * xi_ch
        state = state * chunk_decay[None, :, None, None] + jnp.einsum('bchd,bche,hc->bhde', k_n, v_n, zeta)
        return state, cross

    s0 = jnp.zeros((B, H, dk, dv), jnp.float32)
    xs = (jnp.moveaxis(qc, 1, 0), jnp.moveaxis(kc, 1, 0), jnp.moveaxis(vc, 1, 0))
    _, y_cross = lax.scan(step, s0, xs)
    y = y_intra + jnp.moveaxis(y_cross, 0, 1)
    return y.reshape(B, S, H, dv)


def mla_block_attention(q_nope, q_pe, k_nope, k_pe, v):
    B, S, H, _ = q_nope.shape
    n_blocks = S // Q_BLOCK
    scale = (MLA_NOPE + MLA_ROPE) ** -0.5
    key_chunk = jnp.arange(S) // CHUNK

    def to_blocks(t):
        return jnp.moveaxis(t.reshape(B, n_blocks, Q_BLOCK, *t.shape[2:]), 1, 0)

    def one_block(args):
        blk, qn, qp = args
        s = (jnp.einsum('bqhd,bkhd->bhqk', qn, k_nope)
             + jnp.einsum('bqhr,bkr->bhqk', qp, k_pe)).astype(jnp.float32) * scale
        q_chunk = (blk * Q_BLOCK + jnp.arange(Q_BLOCK)) // CHUNK
        mask = key_chunk[None, :] <= q_chunk[:, None]
        s = jnp.where(mask[None, None], s, -jnp.inf)
        probs = jax.nn.softmax(s, axis=-1).astype(v.dtype)
        return jnp.einsum('bhqk,bkhe->bqhe', probs, v)

    o = lax.map(one_block, (jnp.arange(n_blocks), to_blocks(q_nope), to_blocks(q_pe)))
    return jnp.moveaxis(o, 0, 1).reshape(B, S, H * v.shape[-1])


def hybrid_mixer(h, positions, w_in, ret_gn_g, w_ret_o, q_norm_g, kv_norm_g,
                 w_uq, w_ukv, w_mla_o, w_out):
    B, S, _ = h.shape
    proj = h @ w_in
    (r_q, r_k, r_v, r_g, c_q, c_kv, k_pe_raw, gate_ret, gate_mla) = jnp.split(
        proj, np.cumsum(SPLITS)[:-1], axis=-1)

    rq = rope(r_q.reshape(B, S, RET_HEADS, RET_DK), positions)
    rk = rope(r_k.reshape(B, S, RET_HEADS, RET_DK), positions) * (RET_DK ** -0.5)
    rv = r_v.reshape(B, S, RET_HEADS, RET_DV)
    y = chunk_retention(rq.astype(jnp.float32), rk.astype(jnp.float32), rv.astype(jnp.float32))
    mu = jnp.mean(y, axis=-1, keepdims=True)
    var = jnp.mean(jnp.square(y - mu), axis=-1, keepdims=True)
    y = ((y - mu) * lax.rsqrt(var + EPS)).reshape(B, S, RET_V) * ret_gn_g.astype(jnp.float32)
    y = (jax.nn.silu(r_g.astype(jnp.float32)) * y).astype(h.dtype)
    y_ret = y @ w_ret_o

    q = (rms_norm(c_q, q_norm_g) @ w_uq).reshape(B, S, MLA_HEADS, MLA_NOPE + MLA_ROPE)
    q_nope, q_pe = q[..., :MLA_NOPE], rope(q[..., MLA_NOPE:], positions)
    kv = (rms_norm(c_kv, kv_norm_g) @ w_ukv).reshape(B, S, MLA_HEADS, MLA_NOPE + MLA_DV)
    k_nope, v = kv[..., :MLA_NOPE], kv[..., MLA_NOPE:]
    k_pe = rope(k_pe_raw[:, :, None, :], positions)[:, :, 0, :]
    y_mla = mla_block_attention(q_nope, q_pe, k_nope, k_pe, v) @ w_mla_o

    mix = jax.nn.sigmoid(gate_ret) * y_ret + jax.nn.sigmoid(gate_mla) * y_mla
    return mix @ w_out


def setup_inputs(seed: int = 0) -> dict:
    key = jax.random.key(seed)
    ks = jax.random.split(key, 24)
    f32 = jnp.float32

    def nrm(k, shape, scale):
        return jax.random.normal(k, shape, f32) * scale

    offset = jax.random.randint(ks[2], (BATCH, 1), 0, 4096, dtype=jnp.int32)
    positions = offset + jnp.arange(SEQ, dtype=jnp.int32)[None, :]
    return {
        "x": nrm(ks[0], (BATCH, SEQ, D_MODEL), 1.0),
        "p": nrm(ks[1], (DEPTH, BATCH, SEQ, D_PLE), 1.0),
        "positions": positions,
        "ln_g": 1.0 + nrm(ks[3], (DEPTH, N_LN, D_MODEL), 0.02),
        "ln_b": nrm(ks[4], (DEPTH, N_LN, D_MODEL), 0.02),
        "ffn1_w_in": nrm(ks[5], (DEPTH, D_MODEL, 2 * D_FF), D_MODEL ** -0.5),
        "ffn1_w_out": nrm(ks[6], (DEPTH, D_FF, D_MODEL), DEEPNORM_BETA * D_FF ** -0.5),
        "w_in": nrm(ks[7], (DEPTH, D_MODEL, D_IN_TOTAL), D_MODEL ** -0.5),
        "ret_gn_g": 1.0 + nrm(ks[8], (DEPTH, RET_V), 0.02),
        "w_ret_o": nrm(ks[9], (DEPTH, RET_V, D_MODEL), DEEPNORM_BETA * RET_V ** -0.5),
        "q_norm_g": 1.0 + nrm(ks[10], (DEPTH, Q_LORA), 0.02),
        "kv_norm_g": 1.0 + nrm(ks[11], (DEPTH, KV_LORA), 0.02),
        "w_uq": nrm(ks[12], (DEPTH, Q_LORA, MLA_HEADS * (MLA_NOPE + MLA_ROPE)), Q_LORA ** -0.5),
        "w_ukv": nrm(ks[13], (DEPTH, KV_LORA, MLA_HEADS * (MLA_NOPE + MLA_DV)), KV_LORA ** -0.5),
        "w_mla_o": nrm(ks[14], (DEPTH, MLA_HEADS * MLA_DV, D_MODEL), DEEPNORM_BETA * (MLA_HEADS * MLA_DV) ** -0.5),
        "w_out": nrm(ks[15], (DEPTH, D_MODEL, D_MODEL), DEEPNORM_BETA * D_MODEL ** -0.5),
        "ffn2_w_in": nrm(ks[16], (DEPTH, D_MODEL, 2 * D_FF), D_MODEL ** -0.5),
        "ffn2_w_out": nrm(ks[17], (DEPTH, D_FF, D_MODEL), DEEPNORM_BETA * D_FF ** -0.5),
        "ple_w_gate": nrm(ks[18], (DEPTH, D_MODEL, D_MODEL), D_MODEL ** -0.5),
        "ple_w_proj": nrm(ks[19], (DEPTH, D_PLE, D_MODEL), DEEPNORM_BETA * D_PLE ** -0.5),
    }


def reference(x, p, positions, ln_g, ln_b, ffn1_w_in, ffn1_w_out, w_in, ret_gn_g,
              w_ret_o, q_norm_g, kv_norm_g, w_uq, w_ukv, w_mla_o, w_out,
              ffn2_w_in, ffn2_w_out, ple_w_gate, ple_w_proj):
    h = x
    for i in range(DEPTH):
        h = layer_norm(DEEPNORM_ALPHA * h + 0.5 * swiglu_ffn(h, ffn1_w_in[i], ffn1_w_out[i]),
                       ln_g[i, 0], ln_b[i, 0])
        mixed = hybrid_mixer(h, positions, w_in[i], ret_gn_g[i], w_ret_o[i], q_norm_g[i],
                             kv_norm_g[i], w_uq[i], w_ukv[i], w_mla_o[i], w_out[i])
        h = layer_norm(DEEPNORM_ALPHA * h + mixed, ln_g[i, 1], ln_b[i, 1])
        h = layer_norm(DEEPNORM_ALPHA * h + 0.5 * swiglu_ffn(h, ffn2_w_in[i], ffn2_w_out[i]),
                       ln_g[i, 2], ln_b[i, 2])
        ple = jax.nn.sigmoid(h @ ple_w_gate[i]) * (p[i] @ ple_w_proj[i])
        h = layer_norm(DEEPNORM_ALPHA * h + ple, ln_g[i, 3], ln_b[i, 3])
    return h
```

```python
import math
import os
KDBG = int(os.environ.get('KDBG', '9'))
from contextlib import ExitStack

import numpy as np
import concourse.bass as bass
import concourse.mybir as mybir
from concourse.bass_utils import run_bass_kernel_spmd

F32 = mybir.dt.float32
BF16 = mybir.dt.bfloat16
I32 = mybir.dt.int32
AF = mybir.ActivationFunctionType
ALU = mybir.AluOpType
AX = mybir.AxisListType

D = 1024
DFF = 2816
NFC = DFF // 128
PIECES = [(0, 4), (4, 8), (8, 12), (12, 16), (16, 19), (19, 22)]
ALPHA = 2.0 ** 0.25
EPS = 1e-5
TWO_PI = 2.0 * math.pi
PI_SAFE = 3.1415925
CW1 = 6.28125
CW2 = TWO_PI - 6.28125
MIXW = 1664 + 1024
ATT_SCALE = 192.0 ** -0.5


class _Op:
    __slots__ = ("eng", "fn", "deps", "need_inc", "inc_value", "dsem", "dinc", "phase")

    def __init__(self, eng, fn, phase):
        self.eng = eng
        self.fn = fn
        self.deps = []
        self.need_inc = False
        self.inc_value = None
        self.dsem = None
        self.dinc = 16
        self.phase = phase


class Sched:
    ENG = ("pe", "act", "dve", "pool", "sp")

    def __init__(self, nc, stack):
        self.nc = nc
        self.stack = stack
        self.esem = {e: stack.enter_context(nc.semaphore("es_" + e)) for e in self.ENG}
        self.bar = stack.enter_context(nc.semaphore("bar"))
        self.ecount = {e: 0 for e in self.ENG}
        self.ops = []
        self.last_write = {}
        self.readers = {}
        self.dsems = {}
        self.phase = 0
        self.waited = {e: {} for e in self.ENG}
        self.nsem = 0

    def _dep(self, op, tok, raw):
        if tok is None:
            return
        if tok[0] == "op":
            src = tok[1]
            if src.phase != self.phase:
                return
            if src.eng == op.eng:
                if op.eng in ("pe", "sp") or not raw:
                    return
            src.need_inc = True
            op.deps.append(("op", src))
        else:
            _, key, phase = tok
            if phase != self.phase:
                return
            sem, total = self.dsems[key]
            op.deps.append(("dma", sem, total))

    def _track(self, op, tok, reads, writes):
        for k in reads:
            self._dep(op, self.last_write.get(k), True)
        for k in writes:
            self._dep(op, self.last_write.get(k), True)
            for r in self.readers.get(k, ()):
                self._dep(op, r, False)
        for k in reads:
            self.readers.setdefault(k, []).append(tok)
        for k in writes:
            self.last_write[k] = tok
            self.readers[k] = []

    def op(self, eng, fn, reads=(), writes=()):
        o = _Op(eng, fn, self.phase)
        self._track(o, ("op", o), reads, writes)
        self.ops.append(o)
        return o

    def dma(self, eng, fn, key, reads=(), inc=16, semkey=None):
        o = _Op(eng, fn, self.phase)
        if semkey is None:
            semkey = key
        if semkey not in self.dsems:
            self.nsem += 1
            self.dsems[semkey] = [self.stack.enter_context(self.nc.semaphore("d%d" % self.nsem)), 0]
        ent = self.dsems[semkey]
        tok = ("dma", semkey, self.phase)
        self._track(o, tok, reads, (key,))
        ent[1] += inc
        o.dsem = ent[0]
        o.dinc = inc
        self.ops.append(o)
        return o

    def flush(self):
        nc = self.nc
        ops = self.ops
        self.ops = []
        last = {}
        for o in ops:
            if o.dsem is None:
                last[o.eng] = o
        for e in ("pe", "act", "dve", "pool"):
            if e in last:
                last[e].need_inc = True
        for o in ops:
            if o.need_inc and o.dsem is None:
                self.ecount[o.eng] += 1
                o.inc_value = self.ecount[o.eng]
            elif o.need_inc:
                raise AssertionError("dma op used as engine token")
        self.phase += 1
        bar_val = self.phase
        final_waits = [(self.esem[e], self.ecount[e]) for e in ("pe", "act", "dve", "pool") if self.ecount[e]]
        final_waits += [(s, t) for (s, t) in self.dsems.values() if t]
        by = {e: [o for o in ops if o.eng == e] for e in self.ENG}
        objs = {"pe": nc.tensor, "act": nc.scalar, "dve": nc.vector, "pool": nc.gpsimd, "sp": nc.sync}

        def body(e):
            eo = objs[e]
            wd = self.waited[e]

            def wait(sem, val):
                if wd.get(id(sem), 0) < val:
                    eo.wait_ge(sem, val)
                    wd[id(sem)] = val

            for o in by[e]:
                for d in o.deps:
                    if d[0] == "op":
                        wait(self.esem[d[1].eng], d[1].inc_value)
                    else:
                        wait(d[1], d[2])
                ins = o.fn()
                if o.dsem is not None:
                    if o.dinc == 16:
                        ins.then_inc(o.dsem, 16)
                    else:
                        ins.then_inc(o.dsem)
                elif o.need_inc:
                    ins.then_inc(self.esem[e], 1)
            if e == "sp":
                for sem, val in final_waits:
                    wait(sem, val)
                eo.nop().then_inc(self.bar, 1)
            else:
                wait(self.bar, bar_val)

        self._simulate(by, final_waits, bar_val)
        with nc.named_scope("ph%d" % self.phase), nc.Block() as block:
            block.tensor(lambda _e: body("pe"))
            block.scalar(lambda _e: body("act"))
            block.vector(lambda _e: body("dve"))
            block.gpsimd(lambda _e: body("pool"))
            block.sync(lambda _e: body("sp"))


def _sched_simulate(self, by, final_waits, bar_val):
    semv = dict(getattr(self, "_semv", {}))
    pc = {e: 0 for e in self.ENG}
    n = {e: len(by[e]) + 1 for e in self.ENG}

    def ready(e, i):
        if i < len(by[e]):
            o = by[e][i]
            for d in o.deps:
                if d[0] == "op":
                    if semv.get(id(self.esem[d[1].eng]), 0) < d[1].inc_value:
                        return False
                elif semv.get(id(d[1]), 0) < d[2]:
                    return False
            return True
        if e == "sp":
            return all(semv.get(id(sm), 0) >= v for sm, v in final_waits)
        return semv.get(id(self.bar), 0) >= bar_val

    def fire(e, i):
        if i < len(by[e]):
            o = by[e][i]
            if o.dsem is not None:
                semv[id(o.dsem)] = semv.get(id(o.dsem), 0) + o.dinc
            elif o.need_inc:
                semv[id(self.esem[e])] = semv.get(id(self.esem[e]), 0) + 1
        elif e == "sp":
            semv[id(self.bar)] = semv.get(id(self.bar), 0) + 1

    progress = True
    while progress:
        progress = False
        for e in self.ENG:
            while pc[e] < n[e] and ready(e, pc[e]):
                fire(e, pc[e])
                pc[e] += 1
                progress = True
    stuck = {e: (pc[e], n[e]) for e in self.ENG if pc[e] < n[e]}
    if stuck:
        raise AssertionError("scheduler deadlock: %r" % (stuck,))
    self._semv = semv


Sched._simulate = _sched_simulate


def build_program(S, stop_after=None):
    T = S // 4
    NT = T // 128
    NG = T // 512
    NB = S // 512
    nc = bass.Bass("TRN2", target_bir_lowering=False)

    def din(name, shape, dt=F32):
        return nc.dram_tensor(name, list(shape), dt, kind="ExternalInput").ap()

    x_d = din("x", [T, D])
    p_d = din("p", [T, 256])
    pos_d = din("pos", [128, S], I32)
    lnp_d = din("lnp", [8, 128, D])
    w1_d = [din("ffn1_w1", [128, 8, 2 * DFF]), din("ffn2_w1", [128, 8, 2 * DFF])]
    w2_d = [din("ffn1_w2", [128, NFC, D]), din("ffn2_w2", [128, NFC, D])]
    wmix_d = din("wmix", [128, 8, MIXW])
    wgate_d = din("wgate", [128, 8, 2048])
    wuq_d = din("wuq", [128, 2, 512])
    wukv_d = din("wukv", [128, 2, 512])
    ng_d = din("ng", [128, 4])
    gng_d = din("gng", [128, 512])
    wro_d = din("wro", [128, 16, D])
    wmo_d = din("wmo", [128, 8, D])
    wout_d = din("wout", [128, 8, D])
    wpg_d = din("wpg", [128, 8, D])
    wpp_d = din("wpp", [128, 2, D])
    ident_d = din("ident", [128, 128])
    cst_d = din("cst", [128, 8])
    dtab_d = din("dtab", [2, 128, 4, 512])
    xitab_d = din("xitab", [2, 128, 512])
    zeta_d = din("zeta", [128, 8])
    out_d = nc.dram_tensor("out", [T, D], F32, kind="ExternalOutput").ap()

    h1T_src = [nc.dram_tensor("h1T_src%d" % g, [D, 512], BF16, kind="Internal").ap() for g in range(NG)]
    h1T_all = [nc.dram_tensor("h1T_all%d" % g, [4 * D, 512], BF16, kind="Internal").ap() for g in range(NG)]
    h_spill = nc.dram_tensor("h_spill", [T, D], F32, kind="Internal").ap()
    mix_src = [nc.dram_tensor("mix_src%d" % i, [512, 768], BF16, kind="Internal").ap() for i in range(NB)]
    mix_all = [nc.dram_tensor("mix_all%d" % i, [2048, 768], BF16, kind="Internal").ap() for i in range(NB)]
    mix_comb = nc.dram_tensor("mix_comb", [NB * 2048, 768], BF16, kind="Internal").ap()
    mixT_d = nc.dram_tensor("mixT_d", [D, T], BF16, kind="Internal").ap()
    mix_loc = nc.dram_tensor("mix_loc", [4, T, 768], BF16, kind="Internal").ap()
    RG = [[0, 1, 2, 3], [4, 5, 6, 7]]

    with ExitStack() as top:
        sc = Sched(nc, top)

        uid = [0]

        def sb(name, shape, dt, stack=top):
            uid[0] += 1
            return stack.enter_context(nc.sbuf_tensor("sb%d_%s" % (uid[0], name), list(shape), dt))

        def ps(name, shape, dt, stack):
            uid[0] += 1
            return stack.enter_context(nc.psum_tensor("ps%d_%s" % (uid[0], name), list(shape), dt))

        ident = sb("ident", [128, 128], BF16)
        sc.dma("pool", lambda: nc.gpsimd.dma_start(out=ident[:], in_=ident_d), "ident")
        H = {}
        hst = ExitStack()
        H["h"] = sb("h", [128, NT, D], F32, hst)

        def layer_norm(st, tiles, ln_idx, lng, lnb, junk, stat, out_fn=None):
            tiles = list(tiles)
            h = H["h"]
            ks = "lnstat"
            for t in tiles:
                ht = h[:, t, :]
                k = ("h", t)
                sc.op("dve", lambda ht=ht, t=t: nc.vector.reduce_sum(stat[:, 0, t:t + 1], ht, AX.X), [k], [("s1", t)])
                sc.op("act", lambda ht=ht, t=t: nc.scalar.activation(junk[:], ht, AF.Square, accum_out=stat[:, 1, t:t + 1]),
                      [k], [("s2", t), "lnjunk"])
            t0, t1 = tiles[0], tiles[-1] + 1
            S1, S2, MEAN, VAR, RSTD, NB = [stat[:, j, t0:t1] for j in range(6)]
            rd = [("s1", t) for t in tiles] + [("s2", t) for t in tiles]
            sc.op("dve", lambda: nc.vector.tensor_scalar(MEAN, S1, 1.0 / D, None, ALU.mult), rd, [ks])
            sc.op("dve", lambda: nc.vector.tensor_tensor(VAR, MEAN, MEAN, ALU.mult), [ks], [ks])
            sc.op("dve", lambda: nc.vector.scalar_tensor_tensor(VAR, S2, 1.0 / D, VAR, ALU.mult, ALU.subtract), [ks] + rd, [ks])
            sc.op("dve", lambda: nc.vector.tensor_scalar(VAR, VAR, EPS, None, ALU.add), [ks], [ks])
            sc.op("act", lambda: nc.scalar.sqrt(RSTD, VAR), [ks], [ks])
            sc.op("dve", lambda: nc.vector.reciprocal(RSTD, RSTD), [ks], [ks])
            sc.op("dve", lambda: nc.vector.scalar_tensor_tensor(NB, MEAN, -1.0, RSTD, ALU.mult, ALU.mult), [ks], [ks])
            for t in tiles:
                ht = h[:, t, :]
                k = ("h", t)
                sc.op("act", lambda ht=ht, t=t: nc.scalar.activation(ht, ht, AF.Identity, bias=stat[:, 5, t:t + 1], scale=stat[:, 4, t:t + 1]),
                      [k, ks], [k])
                sc.op("dve", lambda ht=ht: nc.vector.tensor_tensor(ht, ht, lng[:], ALU.mult), [k, "lng"], [k])
                sc.op("pool", lambda ht=ht: nc.gpsimd.tensor_tensor(ht, ht, lnb[:], ALU.add), [k, "lnb"], [k])
                if out_fn is not None:
                    out_fn(t)

        def make_xT(st, xT, hb, tp, scale_after):
            h = H["h"]
            for t in range(NT):
                b = t % 2
                k = ("h", t)
                sc.op("act", lambda t=t, b=b: nc.scalar.copy(hb[b][:], h[:, t, :]), [k], [("hb", b)])
                for kc in range(8):
                    sc.op("pe", lambda b=b, kc=kc: nc.tensor.transpose(tp[b][:, kc, :], hb[b][:, kc * 128:(kc + 1) * 128], ident[:]),
                          [("hb", b), "ident"], [("tp", b)])
                sc.op("dve", lambda t=t, b=b: nc.vector.tensor_copy(xT[:, :, t * 128:(t + 1) * 128], tp[b][:]),
                      [("tp", b)], [("xT", t // 4)])
                if scale_after:
                    sc.op("pool", lambda t=t: nc.gpsimd.tensor_scalar(h[:, t, :], h[:, t, :], ALPHA, None, ALU.mult), [k], [k])

        def ffn_phase(fi, ln_idx, after_tile=None):
            h = H["h"]
            with ExitStack() as st:
                xT = sb("xT", [128, 8, T], BF16, st)
                hb = [sb("hb%d" % i, [128, D], BF16, st) for i in range(2)]
                w1s = [sb("w1s%d" % i, [128, 8, 1024], BF16, st) for i in range(2)]
                w2s = [sb("w2s%d" % i, [128, 4, D], BF16, st) for i in range(2)]
                hT = [sb("hT%d" % i, [128, 4, 512], BF16, st) for i in range(2)]
                sgs = [sb("sgs%d" % i, [128, 512], F32, st) for i in range(2)]
                lng = sb("lng", [128, D], F32, st)
                lnb = sb("lnb", [128, D], F32, st)
                junk = sb("junk", [128, D], F32, st)
                stat = sb("stat", [128, 8, NT], F32, st)
                tp = [ps("tp%d" % i, [128, 8, 128], BF16, st) for i in range(2)]
                pg = [ps("pg%d" % i, [128, 512], F32, st) for i in range(2)]
                po = [ps("po%d" % i, [128, 512], F32, st) for i in range(4)]
                sc.dma("sp", lambda: nc.sync.dma_start(out=lng[:], in_=lnp_d[ln_idx]), "lng")
                sc.dma("sp", lambda: nc.sync.dma_start(out=lnb[:], in_=lnp_d[4 + ln_idx]), "lnb")

                def load_piece(pi):
                    c0, c1 = PIECES[pi]
                    P = c1 - c0
                    slot = pi % 2
                    sc.dma("pool", lambda: nc.gpsimd.dma_start(out=w1s[slot][:, :, 0:2 * P * 128],
                                                               in_=w1_d[fi][:, :, 2 * c0 * 128:2 * c1 * 128]), ("w1s", slot))
                    sc.dma("pool", lambda: nc.gpsimd.dma_start(out=w2s[slot][:, 0:P, :], in_=w2_d[fi][:, c0:c1, :]), ("w2s", slot))

                load_piece(0)
                make_xT(st, xT, hb, tp, True)
                steps = [(pi, g) for pi in range(len(PIECES)) for g in range(NG)]
                cnt = [0, 0]

                def gu(si):
                    pi, g = steps[si]
                    c0, c1 = PIECES[pi]
                    P = c1 - c0
                    slot = pi % 2
                    hbuf = si % 2
                    for ci in range(P):
                        for which in range(2):
                            b = cnt[0] % 2
                            cnt[0] += 1
                            col = (which * P + ci) * 128
                            for kc in range(8):
                                sc.op("pe", lambda b=b, kc=kc, col=col, slot=slot, g=g: nc.tensor.matmul(
                                    pg[b][:], w1s[slot][:, kc, col:col + 128], xT[:, kc, g * 512:(g + 1) * 512],
                                    start=(kc == 0), stop=(kc == 7)),
                                    [("w1s", slot), ("xT", g)], [("pg", b)])
                            if which == 0:
                                sb_ = ci % 2
                                sc.op("act", lambda b=b, sb_=sb_: nc.scalar.activation(sgs[sb_][:], pg[b][:], AF.Silu),
                                      [("pg", b)], [("sgs", sb_)])
                            else:
                                sc.op("dve", lambda b=b, sb_=sb_, ci=ci, hbuf=hbuf: nc.vector.tensor_tensor(
                                    hT[hbuf][:, ci, :], sgs[sb_][:], pg[b][:], ALU.mult),
                                    [("pg", b), ("sgs", sb_)], [("hT", hbuf)])

                def outp(si):
                    pi, g = steps[si]
                    c0, c1 = PIECES[pi]
                    P = c1 - c0
                    slot = pi % 2
                    hbuf = si % 2
                    for tt in range(4):
                        t = g * 4 + tt
                        for half in range(2):
                            b = cnt[1] % 4
                            cnt[1] += 1
                            for ci in range(P):
                                sc.op("pe", lambda b=b, ci=ci, tt=tt, half=half, slot=slot, hbuf=hbuf: nc.tensor.matmul(
                                    po[b][:], hT[hbuf][:, ci, tt * 128:(tt + 1) * 128], w2s[slot][:, ci, half * 512:(half + 1) * 512],
                                    start=(ci == 0), stop=(ci == P - 1)),
                                    [("hT", hbuf), ("w2s", slot)], [("po", b)])
                            hs = h[:, t, half * 512:(half + 1) * 512]
                            eng = "dve" if half == 0 else "pool"
                            if eng == "dve":
                                sc.op("dve", lambda b=b, hs=hs: nc.vector.scalar_tensor_tensor(hs, po[b][:], 0.5, hs, ALU.mult, ALU.add),
                                      [("po", b), ("h", t)], [("h", t)])
                            else:
                                sc.op("dve", lambda b=b, hs=hs: nc.vector.scalar_tensor_tensor(hs, po[b][:], 0.5, hs, ALU.mult, ALU.add),
                                      [("po", b), ("h", t)], [("h", t)])

                for si in range(len(steps) + 1):
                    if si < len(steps):
                        gu(si)
                    if si >= 1:
                        outp(si - 1)
                    if si < len(steps) and steps[si][1] == 0 and steps[si][0] + 1 < len(PIECES):
                        load_piece(steps[si][0] + 1)
                layer_norm(st, range(NT), ln_idx, lng, lnb, junk, stat, after_tile)
                sc.flush()

        h = H["h"]
        for t in range(NT):
            sc.dma("sp", lambda t=t, h=h: nc.sync.dma_start(out=h[:, t, :], in_=x_d[t * 128:(t + 1) * 128, :]), ("h", t), semkey=("hload", t // 4))
        ffn_phase(0, 0)

        def dump_h():
            h = H["h"]
            if KDBG in (21, 22, 23, 24):
                return
            for t in range(NT):
                sc.dma("sp", lambda t=t: nc.sync.dma_start(out=out_d[t * 128:(t + 1) * 128, :], in_=h[:, t, :]), ("out", t), [("h", t)], semkey="outst")
            sc.flush()

        if stop_after == "A":
            dump_h()
            hst.close()
            return nc

        with ExitStack() as st:
            xT = sb("xT", [128, 8, T], BF16, st)
            hb = [sb("hb%d" % i, [128, D], BF16, st) for i in range(2)]
            tp = [ps("tp%d" % i, [128, 8, 128], BF16, st) for i in range(2)]
            for t in range(NT):
                sc.dma("sp", lambda t=t: nc.sync.dma_start(out=h_spill[t * 128:(t + 1) * 128, :], in_=h[:, t, :]), ("hsp", t), [("h", t)], semkey="hsp")
            make_xT(st, xT, hb, tp, False)
            for g in range(NG):
                for kc in range(8):
                    sc.dma("sp", lambda kc=kc, g=g: nc.sync.dma_start(out=h1T_src[g][kc * 128:(kc + 1) * 128, :], in_=xT[:, kc, g * 512:(g + 1) * 512]),
                           ("h1T_src", g), [("xT", g)])
                sc.dma("pool", lambda g=g: nc.gpsimd.collective_compute("AllGather", ALU.bypass, replica_groups=RG,
                                                                        ins=[h1T_src[g].opt()], outs=[h1T_all[g].opt()]),
                       ("h1T_all", g), [("h1T_src", g)], inc=1, semkey="h1T_all")
            sc.flush()
        if stop_after == "B":
            dump_h()
            hst.close()
            return nc
        hst.close()

        def load_hblk(hblk, tb):
            rr, j = tb // NG, tb % NG
            src = h1T_all[j][rr * D:(rr + 1) * D, :].rearrange("(kc p) t -> p kc t", p=128)
            sc.dma("sp", lambda: nc.sync.dma_start(out=hblk[tb % 2][:], in_=src), ("hblk", tb % 2), [("h1T_all", j)])

        def rope_tables(tb, posi, posf, ang, r1, cosT, sinT, cst, col, kcos, ksin, ki):
            sc.dma("sp", lambda: nc.sync.dma_start(out=posi[:], in_=pos_d[:, tb * 512:(tb + 1) * 512]), "posi")
            sc.op("dve", lambda: nc.vector.tensor_copy(posf[:], posi[:]), ["posi"], ["posf"])
            sc.op("dve", lambda: nc.vector.tensor_scalar(ang[:], posf[:], cst[:, col:col + 1], None, ALU.mult), ["posf", "cst"], ["ang"])
            for which in range(2):
                if which == 1:
                    sc.op("dve", lambda: nc.vector.tensor_scalar(ang[:], ang[:], 0.5 * math.pi, None, ALU.add), ["ang"], ["ang"])
                sc.op("dve", lambda: nc.vector.tensor_scalar(r1[:], ang[:], 1.0 / TWO_PI, None, ALU.mult), ["ang"], ["r1"])
                sc.op("dve", lambda: nc.vector.tensor_copy(ki[:], r1[:]), ["r1"], ["ki"])
                sc.op("dve", lambda: nc.vector.tensor_copy(r1[:], ki[:]), ["ki"], ["r1"])
                sc.op("dve", lambda: nc.vector.scalar_tensor_tensor(posf[:], r1[:], -CW1, ang[:], ALU.mult, ALU.add), ["r1", "ang"], ["posf"])
                sc.op("dve", lambda: nc.vector.scalar_tensor_tensor(posf[:], r1[:], -CW2, posf[:], ALU.mult, ALU.add), ["r1", "posf"], ["posf"])
                sc.op("dve", lambda: nc.vector.tensor_scalar(posf[:], posf[:], PI_SAFE, -PI_SAFE, ALU.min, ALU.max), ["posf"], ["posf"])
                if which == 0:
                    sc.op("act", lambda: nc.scalar.activation(sinT[:], posf[:], AF.Sin, scale=cst[:, 2 + 2 * col:3 + 2 * col]), ["posf", "cst"], [ksin])
                else:
                    sc.op("act", lambda: nc.scalar.activation(cosT[:], posf[:], AF.Sin), ["posf"], [kcos])

        with ExitStack() as st:
            wr = sb("wr", [128, 8, 2048], BF16, st)
            dtab = sb("dtab", [128, 2, 4, 512], F32, st)
            xitab = sb("xitab", [128, 2, 512], F32, st)
            zeta = sb("zeta", [128, 8], F32, st)
            gng = sb("gng", [128, 512], F32, st)
            cst = sb("cst", [128, 8], F32, st)
            hblk = [sb("hblk%d" % i, [128, 8, 512], BF16, st) for i in range(2)]
            posi = sb("posi", [128, 512], I32, st)
            ki = sb("ki", [128, 512], I32, st)
            posf = sb("posf", [128, 512], F32, st)
            ang = sb("ang", [128, 512], F32, st)
            r1 = sb("r1", [128, 512], F32, st)
            cosR2 = [sb("cosR%d" % i, [128, 512], F32, st) for i in range(2)]
            sinR2 = [sb("sinR%d" % i, [128, 512], F32, st) for i in range(2)]
            t1 = [sb("t1_%d" % i, [128, 512], F32, st) for i in range(2)]
            t2 = [sb("t2_%d" % i, [128, 512], F32, st) for i in range(2)]
            qT = sb("qT", [128, 2, 512], BF16, st)
            qxT = sb("qxT", [128, 2, 512], BF16, st)
            kT = sb("kT", [128, 2, 512], BF16, st)
            ktok = sb("ktok", [128, 4, 2, 128], BF16, st)
            vtok = sb("vtok", [128, 4, 512], BF16, st)
            rgs = sb("rgs", [128, 4, 512], BF16, st)
            PT = sb("PT", [128, 2, 4, 512], BF16, st)
            Sst = sb("Sst", [128, 2, 256], F32, st)
            Sbf = sb("Sbf", [128, 2, 256], BF16, st)
            ybuf = sb("ybuf", [128, 8, 256], F32, st)
            yjunk = sb("yjunk", [128, 256], F32, st)
            ystat = sb("ystat", [128, 8, 8], F32, st)
            mixo = sb("mixo", [128, 4, 512], BF16, st)
            pA = [ps("pA%d" % i, [128, 512], F32, st) for i in range(2)]
            pB = [ps("pB%d" % i, [128, 512], F32, st) for i in range(2)]
            pY = [ps("pY%d" % i, [128, 512], F32, st) for i in range(2)]
            pKV = ps("pKV", [128, 512], F32, st)
            pTR = ps("pTR", [128, 8, 128], BF16, st)

            sc.dma("pool", lambda: nc.gpsimd.dma_start(out=wr[:, :, 0:1024], in_=wmix_d[:, :, 0:1024]), "wr")
            sc.dma("pool", lambda: nc.gpsimd.dma_start(out=wr[:, :, 1024:2048], in_=wmix_d[:, :, 1664:2688]), "wr")
            for j in range(2):
                sc.dma("sp", lambda j=j: nc.sync.dma_start(out=dtab[:, j, :, :], in_=dtab_d[j]), "dtab")
                sc.dma("sp", lambda j=j: nc.sync.dma_start(out=xitab[:, j, :], in_=xitab_d[j]), "xitab")
            sc.dma("sp", lambda: nc.sync.dma_start(out=zeta[:], in_=zeta_d), "zeta")
            sc.dma("sp", lambda: nc.sync.dma_start(out=gng[:], in_=gng_d), "gng")
            sc.dma("sp", lambda: nc.sync.dma_start(out=cst[:], in_=cst_d), "cst")
            sc.op("dve", lambda: nc.vector.memset(Sst[:], 0.0), [], ["Sst"])
            sc.op("dve", lambda: nc.vector.memset(Sbf[:], 0.0), [], ["Sbf"])
            sc.op("dve", lambda: nc.vector.memset(ystat[:], 0.0), [], ["ystat"])
            load_hblk(hblk, 0)
            sbank = [0]
            for tb in range(NB):
                hk = ("hblk", tb % 2)
                hb_ = hblk[tb % 2]
                if tb + 1 < NB:
                    load_hblk(hblk, tb + 1)
                sc.op("dve", lambda: nc.vector.memset(ystat[:, 0:2, :], 0.0), [],
                      ["ystat"] + [("ys1", i) for i in range(8)] + [("ys2", i) for i in range(8)])
                cosR, sinR = cosR2[tb % 2], sinR2[tb % 2]
                kcos, ksin = ("cosR", tb % 2), ("sinR", tb % 2)
                if tb == 0:
                    rope_tables(0, posi, posf, ang, r1, cosR2[0], sinR2[0], cst, 0, ("cosR", 0), ("sinR", 0), ki)
                if KDBG < 1:
                    continue
                for i, (qk, hd) in enumerate([(0, 0), (0, 1), (1, 0), (1, 1)]):
                    b = i % 2
                    c0 = (qk * 4 + hd) * 128
                    c1 = (qk * 4 + 2 + hd) * 128
                    for kc in range(8):
                        sc.op("pe", lambda b=b, kc=kc, c0=c0, hb_=hb_: nc.tensor.matmul(pA[b][:], wr[:, kc, c0:c0 + 128], hb_[:, kc, :],
                                                                                  start=(kc == 0), stop=(kc == 7)), ["wr", hk], [("pA", b)])
                    for kc in range(8):
                        sc.op("pe", lambda b=b, kc=kc, c1=c1, hb_=hb_: nc.tensor.matmul(pB[b][:], wr[:, kc, c1:c1 + 128], hb_[:, kc, :],
                                                                                  start=(kc == 0), stop=(kc == 7)), ["wr", hk], [("pB", b)])
                    sc.op("dve", lambda b=b, cosR=cosR: nc.vector.tensor_tensor(t1[b][:], pA[b][:], cosR[:], ALU.mult), [("pA", b), kcos], [("t1", b)])
                    sc.op("dve", lambda b=b, sinR=sinR: nc.vector.tensor_tensor(t2[b][:], pB[b][:], sinR[:], ALU.mult), [("pB", b), ksin], [("t2", b)])
                    dst = (qT if qk == 0 else kT)
                    dk = ("qT" if qk == 0 else "kT", hd)
                    sc.op("pool", lambda b=b, dst=dst, hd=hd: nc.gpsimd.tensor_tensor(dst[:, hd, :], t1[b][:], t2[b][:], ALU.add),
                          [("t1", b), ("t2", b)], [dk])
                    if qk == 0:
                        sc.op("pool", lambda hd=hd: nc.gpsimd.tensor_tensor(qxT[:, hd, :], qT[:, hd, :], xitab[:, hd, :], ALU.mult),
                              [dk, "xitab"], [("qxT", hd)])
                if tb + 1 < NB:
                    rope_tables(tb + 1, posi, posf, ang, r1, cosR2[(tb + 1) % 2], sinR2[(tb + 1) % 2], cst, 0,
                                ("cosR", (tb + 1) % 2), ("sinR", (tb + 1) % 2), ki)
                if KDBG == 30 and tb == 0:
                    sc.dma("sp", lambda: nc.sync.dma_start(out=out_d[0:128, 0:512], in_=cosR2[0][:]), ("out", 0), [("cosR", 0)], semkey="outst")
                    sc.dma("sp", lambda: nc.sync.dma_start(out=out_d[128:256, 0:512], in_=sinR2[0][:]), ("out", 1), [("sinR", 0)], semkey="outst")
                    sc.dma("pool", lambda: nc.gpsimd.dma_start(out=out_d[256:384, 0:512], in_=qT[:, 0, :]), ("out", 2), [("qT", 0)], semkey="outst")
                    sc.dma("pool", lambda: nc.gpsimd.dma_start(out=out_d[384:512, 0:512], in_=kT[:, 0, :]), ("out", 3), [("kT", 0)], semkey="outst")
                for which in range(2):
                    for tt in range(4):
                        b = (which * 4 + tt) % 2
                        for kc in range(8):
                            sc.op("pe", lambda b=b, kc=kc, tt=tt, which=which, hb_=hb_: nc.tensor.matmul(
                                pA[b][:], hb_[:, kc, tt * 128:(tt + 1) * 128], wr[:, kc, 1024 + which * 512:1536 + which * 512],
                                start=(kc == 0), stop=(kc == 7)), ["wr", hk], [("pA", b)])
                        if which == 0:
                            sc.op("act", lambda b=b, tt=tt: nc.scalar.copy(vtok[:, tt, :], pA[b][:]), [("pA", b)], [("vtok", tt)])
                        else:
                            sc.op("act", lambda b=b, tt=tt: nc.scalar.activation(rgs[:, tt, :], pA[b][:], AF.Silu), [("pA", b)], [("rgs", tt)])
                if KDBG < 2:
                    continue
                for hd in range(2):
                    for tt in range(4):
                        sc.op("pe", lambda hd=hd, tt=tt: nc.tensor.transpose(pTR[:, tt, :], kT[:, hd, tt * 128:(tt + 1) * 128], ident[:]),
                              [("kT", hd), "ident"], ["pTR"])
                    for tt in range(4):
                        sc.op("dve", lambda hd=hd, tt=tt: nc.vector.tensor_scalar(ktok[:, tt, hd, :], pTR[:, tt, :],
                                                                                   zeta[:, hd * 4 + tt:hd * 4 + tt + 1], None, ALU.mult),
                              ["pTR", "zeta"], [("ktok", hd)])
                for hd in range(2):
                    for kt in range(4):
                        N = 512 - kt * 128
                        bi = sbank[0] % 4
                        sbank[0] += 1
                        pt_, pk = (pA[bi], ("pA", bi)) if bi < 2 else (pB[bi - 2], ("pB", bi - 2))
                        sc.op("pe", lambda pt_=pt_, hd=hd, kt=kt, N=N: nc.tensor.matmul(
                            pt_[:, 0:N], kT[:, hd, kt * 128:(kt + 1) * 128], qT[:, hd, kt * 128:512], start=True, stop=True),
                            [("kT", hd), ("qT", hd)], [pk])
                        sc.op("dve", lambda pt_=pt_, hd=hd, kt=kt, N=N: nc.vector.tensor_tensor(
                            PT[:, hd, kt, kt * 128:512], pt_[:, 0:N], dtab[:, hd, kt, kt * 128:512], ALU.mult),
                            [pk, "dtab"], [("PT", hd, kt)])
                if KDBG < 3:
                    continue
                for hd in range(2):
                    for nt in range(4):
                        idx = hd * 4 + nt
                        ysl = idx % 2
                        first = True
                        for kt in range(nt + 1):
                            sc.op("pe", lambda hd=hd, nt=nt, kt=kt, ysl=ysl, first=first: nc.tensor.matmul(
                                pY[ysl][:, 0:256], PT[:, hd, kt, nt * 128:(nt + 1) * 128], vtok[:, kt, hd * 256:(hd + 1) * 256],
                                start=first, stop=False), [("PT", hd, kt), ("vtok", kt)], [("pY", ysl)])
                            first = False
                        sc.op("pe", lambda hd=hd, nt=nt, ysl=ysl: nc.tensor.matmul(
                            pY[ysl][:, 0:256], qxT[:, hd, nt * 128:(nt + 1) * 128], Sbf[:, hd, :], start=False, stop=True),
                            [("qxT", hd), ("Sbf", hd)], [("pY", ysl)])
                        sc.op("act", lambda idx=idx, ysl=ysl: nc.scalar.copy(ybuf[:, idx, :], pY[ysl][:, 0:256]),
                              [("pY", ysl)], [("ybuf", idx)])
                        sc.op("dve", lambda idx=idx: nc.vector.reduce_sum(ystat[:, 0, idx:idx + 1], ybuf[:, idx, :], AX.X),
                              [("ybuf", idx)], [("ys1", idx)])
                        sc.op("act", lambda idx=idx: nc.scalar.activation(yjunk[:], ybuf[:, idx, :], AF.Square,
                                                                         accum_out=ystat[:, 1, idx:idx + 1]),
                              [("ybuf", idx)], ["yjunk", ("ys2", idx)])
                if KDBG == 31 and tb == 0:
                    for nt in range(4):
                        sc.dma("sp", lambda nt=nt: nc.sync.dma_start(out=out_d[nt * 128:(nt + 1) * 128, 0:256], in_=ybuf[:, nt, :]),
                               ("out", nt), [("ybuf", nt)], semkey="outst")
                        sc.dma("pool", lambda nt=nt: nc.gpsimd.dma_start(out=out_d[nt * 128:(nt + 1) * 128, 256:768], in_=vtok[:, nt, :]),
                               ("out", nt), [("vtok", nt)], semkey="outst")
                        sc.dma("pool", lambda nt=nt: nc.gpsimd.dma_start(out=out_d[nt * 128:(nt + 1) * 128, 768:1024], in_=rgs[:, nt, 0:256]),
                               ("out", nt), [("rgs", nt)], semkey="outst")
                for hd in range(2):
                    for tt in range(4):
                        sc.op("pe", lambda hd=hd, tt=tt: nc.tensor.matmul(pKV[:, 0:256], ktok[:, tt, hd, :], vtok[:, tt, hd * 256:(hd + 1) * 256],
                                                                          start=(tt == 0), stop=(tt == 3)),
                              [("ktok", hd), ("vtok", tt)], ["pKV"])
                    sc.op("dve", lambda hd=hd: nc.vector.scalar_tensor_tensor(Sst[:, hd, :], Sst[:, hd, :], xitab[:, hd, 511:512],
                                                                             pKV[:, 0:256], ALU.mult, ALU.add),
                          ["pKV", "Sst", "xitab"], ["Sst"])
                    sc.op("act", lambda hd=hd: nc.scalar.copy(Sbf[:, hd, :], Sst[:, hd, :]), ["Sst"], [("Sbf", hd)])
                if KDBG < 4:
                    continue
                rd = [("ys1", i) for i in range(8)] + [("ys2", i) for i in range(8)]
                S1, S2, MEAN, VAR, RSTD, NBI = [ystat[:, j, :] for j in range(6)]
                ks = "ystat"
                sc.op("dve", lambda: nc.vector.tensor_scalar(MEAN, S1, 1.0 / 256, None, ALU.mult), rd + [ks], [ks])
                sc.op("dve", lambda: nc.vector.tensor_tensor(VAR, MEAN, MEAN, ALU.mult), [ks], [ks])
                sc.op("dve", lambda: nc.vector.scalar_tensor_tensor(VAR, S2, 1.0 / 256, VAR, ALU.mult, ALU.subtract), [ks] + rd, [ks])
                sc.op("dve", lambda: nc.vector.tensor_scalar(VAR, VAR, EPS, None, ALU.add), [ks], [ks])
                sc.op("act", lambda: nc.scalar.sqrt(RSTD, VAR), [ks], [ks])
                sc.op("dve", lambda: nc.vector.reciprocal(RSTD, RSTD), [ks], [ks])
                sc.op("dve", lambda: nc.vector.scalar_tensor_tensor(NBI, MEAN, -1.0, RSTD, ALU.mult, ALU.mult), [ks], [ks])
                for hd in range(2):
                    for nt in range(4):
                        idx = hd * 4 + nt
                        sc.op("act", lambda idx=idx: nc.scalar.activation(ybuf[:, idx, :], ybuf[:, idx, :], AF.Identity,
                                                                         bias=ystat[:, 5, idx:idx + 1], scale=ystat[:, 4, idx:idx + 1]),
                              [("ybuf", idx), ks], [("ybuf", idx)])
                        sc.op("dve", lambda idx=idx, hd=hd: nc.vector.tensor_tensor(ybuf[:, idx, :], ybuf[:, idx, :], gng[:, hd * 256:(hd + 1) * 256], ALU.mult),
                              [("ybuf", idx), "gng"], [("ybuf", idx)])
                        sc.op("pool", lambda idx=idx, hd=hd, nt=nt: nc.gpsimd.tensor_tensor(mixo[:, nt, hd * 256:(hd + 1) * 256], ybuf[:, idx, :],
                                                                                            rgs[:, nt, hd * 256:(hd + 1) * 256], ALU.mult),
                              [("ybuf", idx), ("rgs", nt)], ["mixo"])
                sc.dma("sp", lambda tb=tb: nc.sync.dma_start(
                    out=mix_src[tb][:, 0:512].rearrange("(n p) c -> p n c", p=128), in_=mixo[:]),
                    ("mix_src", tb, 0), ["mixo"], semkey="mix_src0")
            sc.flush()

        if stop_after == "C1":
            return nc
        NKT = S // 128
        with ExitStack() as st:
            wml = sb("wml", [128, 8, 640], BF16, st)
            wq_f = sb("wq_f", [128, 2, 512], F32, st)
            wkv_f = sb("wkv_f", [128, 2, 512], F32, st)
            wq = sb("wq", [128, 2, 512], BF16, st)
            wkv = sb("wkv", [128, 2, 512], BF16, st)
            ng = sb("ng", [128, 4], F32, st)
            cst = sb("cst", [128, 8], F32, st)
            onesf = sb("onesf", [128, 128], BF16, st)
            kn = sb("kn", [128, 2, S], BF16, st)
            kpe = sb("kpe", [64, S], BF16, st)
            vaug = sb("vaug", [128, NKT, 2, 130], BF16, st)
            hblk = [sb("hblk%d" % i, [128, 8, 512], BF16, st) for i in range(2)]
            posi = sb("posi", [128, 512], I32, st)
            ki = sb("ki", [128, 512], I32, st)
            posf = sb("posf", [128, 512], F32, st)
            ang = sb("ang", [128, 512], F32, st)
            r1 = sb("r1", [128, 512], F32, st)
            cosM2 = [sb("cosM%d" % i, [128, 512], F32, st) for i in range(2)]
            sinM2 = [sb("sinM%d" % i, [128, 512], F32, st) for i in range(2)]
            latf = sb("latf", [128, 4, 512], F32, st)
            sq = sb("sq", [128, 4, 512], BF16, st)
            latn = sb("latn", [128, 4, 512], BF16, st)
            rstd = sb("rstd", [128, 2, 512], F32, st)
            t1 = [sb("t1_%d" % i, [64, 512], F32, st) for i in range(2)]
            t2 = [sb("t2_%d" % i, [64, 512], F32, st) for i in range(2)]
            qn = sb("qn", [128, 2, 512], BF16, st)
            qpe = sb("qpe", [64, 2, 512], BF16, st)
            PTa = [sb("PTa%d" % i, [128, 512], BF16, st) for i in range(4)]
            rl = sb("rl", [128, 8], F32, st)
            oout = sb("oout", [128, 4, 256], BF16, st)
            pA = [ps("pA%d" % i, [128, 512], F32, st) for i in range(2)]
            pB = [ps("pB%d" % i, [128, 512], F32, st) for i in range(2)]
            pS = [ps("pS%d" % i, [128, 512], F32, st) for i in range(4)]

            sc.dma("pool", lambda: nc.gpsimd.dma_start(out=wml[:], in_=wmix_d[:, :, 1024:1664]), "wml")
            sc.dma("sp", lambda: nc.sync.dma_start(out=wq_f[:], in_=wuq_d), "wq_f")
            sc.dma("sp", lambda: nc.sync.dma_start(out=wkv_f[:], in_=wukv_d), "wkv_f")
            sc.dma("sp", lambda: nc.sync.dma_start(out=ng[:], in_=ng_d), "ng")
            sc.dma("sp", lambda: nc.sync.dma_start(out=cst[:], in_=cst_d), "cst")
            for c in range(2):
                sc.op("dve", lambda c=c: nc.vector.tensor_scalar(wq[:, c, :], wq_f[:, c, :], ng[:, c:c + 1], None, ALU.mult),
                      ["wq_f", "ng"], ["wq"])
                sc.op("dve", lambda c=c: nc.vector.tensor_scalar(wkv[:, c, :], wkv_f[:, c, :], ng[:, 2 + c:3 + c], None, ALU.mult),
                      ["wkv_f", "ng"], ["wkv"])
            sc.op("dve", lambda: nc.vector.memset(onesf[:], 1.0 / 256.0), [], ["onesf"])
            sc.op("pool", lambda: nc.gpsimd.memset(vaug[:], 1.0), [], ["vaug"] + [("vaug", i) for i in range(NB)])
            load_hblk(hblk, 0)
            cntS = [0]
            cntP = [0]
            ACC = [(pA[0], ("pA", 0)), (pA[1], ("pA", 1)), (pB[0], ("pB", 0)), (pB[1], ("pB", 1))]
            for tb in range(NB):
                hk = ("hblk", tb % 2)
                hb_ = hblk[tb % 2]
                g0 = tb * 512
                if tb + 1 < NB:
                    load_hblk(hblk, tb + 1)
                cosM, sinM = cosM2[tb % 2], sinM2[tb % 2]
                kcm, ksm = ("cosM", tb % 2), ("sinM", tb % 2)
                if tb == 0:
                    rope_tables(0, posi, posf, ang, r1, cosM2[0], sinM2[0], cst, 1, ("cosM", 0), ("sinM", 0), ki)
                if KDBG < 5:
                    continue
                for i in range(4):
                    b = i % 2
                    for kc in range(8):
                        sc.op("pe", lambda b=b, kc=kc, i=i, hb_=hb_: nc.tensor.matmul(pA[b][:], wml[:, kc, i * 128:(i + 1) * 128], hb_[:, kc, :],
                                                                                 start=(kc == 0), stop=(kc == 7)), ["wml", hk], [("pA", b)])
                    sc.op("act", lambda b=b, i=i: nc.scalar.copy(latf[:, i, :], pA[b][:]), [("pA", b)], [("latf", i)])
                    sc.op("act", lambda i=i: nc.scalar.activation(sq[:, i, :], latf[:, i, :], AF.Square), [("latf", i)], [("sq", i)])
                for w in range(2):
                    for c in range(2):
                        sc.op("pe", lambda w=w, c=c: nc.tensor.matmul(pB[w][:], onesf[:], sq[:, w * 2 + c, :], start=(c == 0), stop=(c == 1)),
                              ["onesf", ("sq", w * 2 + c)], [("pB", w)])
                    sc.op("dve", lambda w=w: nc.vector.tensor_scalar(rstd[:, w, :], pB[w][:], EPS, None, ALU.add), [("pB", w)], [("rstd", w)])
                    sc.op("act", lambda w=w: nc.scalar.sqrt(rstd[:, w, :], rstd[:, w, :]), [("rstd", w)], [("rstd", w)])
                    sc.op("dve", lambda w=w: nc.vector.reciprocal(rstd[:, w, :], rstd[:, w, :]), [("rstd", w)], [("rstd", w)])
                    if w == 0:
                        sc.op("dve", lambda: nc.vector.tensor_scalar(rstd[:, 0, :], rstd[:, 0, :], ATT_SCALE, None, ALU.mult), [("rstd", 0)], [("rstd", 0)])
                    for c in range(2):
                        i = w * 2 + c
                        sc.op("pool", lambda i=i, w=w: nc.gpsimd.tensor_tensor(latn[:, i, :], latf[:, i, :], rstd[:, w, :], ALU.mult),
                              [("latf", i), ("rstd", w)], [("latn", i)])

                def rope64(psA, psB, kA, kB, dst_ap, dkey, slot):
                    sc.op("dve", lambda cosM=cosM: nc.vector.tensor_tensor(t1[slot][:], psA[0:64, :], cosM[0:64, :], ALU.mult), [kA, kcm], [("t1", slot)])
                    sc.op("dve", lambda sinM=sinM: nc.vector.tensor_tensor(t2[slot][:], psB[0:64, :], sinM[0:64, :], ALU.mult), [kB, ksm], [("t2", slot)])
                    sc.op("pool", lambda: nc.gpsimd.tensor_tensor(dst_ap, t1[slot][:], t2[slot][:], ALU.add), [("t1", slot), ("t2", slot)], [dkey])

                if KDBG < 6:
                    continue
                for kc in range(8):
                    sc.op("pe", lambda kc=kc, hb_=hb_: nc.tensor.matmul(pA[0][0:64, :], wml[:, kc, 512:576], hb_[:, kc, :], start=(kc == 0), stop=(kc == 7)),
                          ["wml", hk], [("pA", 0)])
                for kc in range(8):
                    sc.op("pe", lambda kc=kc, hb_=hb_: nc.tensor.matmul(pB[0][0:64, :], wml[:, kc, 576:640], hb_[:, kc, :], start=(kc == 0), stop=(kc == 7)),
                          ["wml", hk], [("pB", 0)])
                rope64(pA[0], pB[0], ("pA", 0), ("pB", 0), kpe[:, g0:g0 + 512], ("kpe", tb), 0)
                for hd in range(2):
                    for c in range(2):
                        sc.op("pe", lambda hd=hd, c=c: nc.tensor.matmul(pA[1][:], wq[:, c, hd * 256:hd * 256 + 128], latn[:, c, :], start=(c == 0), stop=(c == 1)),
                              ["wq", ("latn", c)], [("pA", 1)])
                    sc.op("act", lambda hd=hd: nc.scalar.copy(qn[:, hd, :], pA[1][:]), [("pA", 1)], [("qn", hd)])
                    for c in range(2):
                        sc.op("pe", lambda hd=hd, c=c: nc.tensor.matmul(pA[0][0:64, :], wq[:, c, hd * 256 + 128:hd * 256 + 192], latn[:, c, :], start=(c == 0), stop=(c == 1)),
                              ["wq", ("latn", c)], [("pA", 0)])
                    for c in range(2):
                        sc.op("pe", lambda hd=hd, c=c: nc.tensor.matmul(pB[0][0:64, :], wq[:, c, hd * 256 + 192:hd * 256 + 256], latn[:, c, :], start=(c == 0), stop=(c == 1)),
                              ["wq", ("latn", c)], [("pB", 0)])
                    rope64(pA[0], pB[0], ("pA", 0), ("pB", 0), qpe[:, hd, :], ("qpe", hd), 1)
                    for c in range(2):
                        sc.op("pe", lambda hd=hd, c=c: nc.tensor.matmul(pB[1][:], wkv[:, c, hd * 128:(hd + 1) * 128], latn[:, 2 + c, :], start=(c == 0), stop=(c == 1)),
                              ["wkv", ("latn", 2 + c)], [("pB", 1)])
                    sc.op("act", lambda hd=hd, g0=g0: nc.scalar.copy(kn[:, hd, g0:g0 + 512], pB[1][:]), [("pB", 1)], [("kn", tb)])
                for tt in range(4):
                    b = tt % 2
                    for c in range(2):
                        sc.op("pe", lambda b=b, tt=tt, c=c: nc.tensor.matmul(pA[b][:, 0:256], latn[:, 2 + c, tt * 128:(tt + 1) * 128], wkv[:, c, 256:512],
                                                                            start=(c == 0), stop=(c == 1)), ["wkv", ("latn", 2 + c)], [("pA", b)])
                    sc.op("act", lambda b=b, tt=tt, tb=tb: nc.scalar.copy(vaug[:, tb * 4 + tt, :, 0:128], pA[b][:, 0:256].rearrange("p (h d) -> p h d", h=2)),
                          [("pA", b)], [("vaug", tb)])
                if KDBG == 41 and tb == 0:
                    sc.dma("pool", lambda: nc.gpsimd.dma_start(out=out_d[0:128, 0:512], in_=latn[:, 2, :]), ("out", 0), [("latn", 2)], semkey="outst")
                    sc.dma("pool", lambda: nc.gpsimd.dma_start(out=out_d[128:256, 0:512], in_=wkv[:, 0, :]), ("out", 1), ["wkv"], semkey="outst")
                    sc.dma("sp", lambda: nc.sync.dma_start(out=out_d[256:384, 0:512], in_=rstd[:, 1, :]), ("out", 2), [("rstd", 1)], semkey="outst")
                    sc.dma("sp", lambda: nc.sync.dma_start(out=out_d[384:512, 0:512], in_=latf[:, 2, :]), ("out", 3), [("latf", 2)], semkey="outst")
                if KDBG == 40 and tb == 0:
                    sc.dma("pool", lambda: nc.gpsimd.dma_start(out=out_d[0:128, 0:512], in_=qn[:, 0, :]), ("out", 0), [("qn", 0)], semkey="outst")
                    sc.dma("pool", lambda: nc.gpsimd.dma_start(out=out_d[128:256, 0:512], in_=kn[:, 0, 0:512]), ("out", 1), [("kn", 0)], semkey="outst")
                    sc.dma("pool", lambda: nc.gpsimd.dma_start(out=out_d[256:320, 0:512], in_=qpe[:, 0, :]), ("out", 2), [("qpe", 0)], semkey="outst")
                    sc.dma("pool", lambda: nc.gpsimd.dma_start(out=out_d[320:384, 0:512], in_=kpe[:, 0:512]), ("out", 3), [("kpe", 0)], semkey="outst")
                    sc.dma("pool", lambda: nc.gpsimd.dma_start(out=out_d[384:512, 0:130], in_=vaug[:, 0, 0, :]), ("out", 4), [("vaug", 0)], semkey="outst")
                    sc.dma("sp", lambda: nc.sync.dma_start(out=out_d[384:512, 512:1024], in_=rstd[:, 0, :]), ("out", 5), [("rstd", 0)], semkey="outst")
                if KDBG < 7:
                    continue
                if tb + 1 < NB:
                    rope_tables(tb + 1, posi, posf, ang, r1, cosM2[(tb + 1) % 2], sinM2[(tb + 1) % 2], cst, 1,
                                ("cosM", (tb + 1) % 2), ("sinM", (tb + 1) % 2), ki)
                nkt = 4 * tb + 4
                tiles = [(hd, kt) for hd in range(2) for kt in range(nkt)]
                LA = 3

                def emit_scores(hd, kt, idx):
                    j = max(0, kt - 4 * tb)
                    qoff = j * 128
                    N = 512 - qoff
                    sbk = idx % 4
                    kb = kt // 4
                    sc.op("pe", lambda: nc.tensor.matmul(
                        pS[sbk][:, 0:N], kn[:, hd, kt * 128:(kt + 1) * 128], qn[:, hd, qoff:512], start=True, stop=False),
                        [("kn", kb), ("qn", hd)], [("pS", sbk)])
                    sc.op("pe", lambda: nc.tensor.matmul(
                        pS[sbk][:, 0:N], kpe[:, kt * 128:(kt + 1) * 128], qpe[:, hd, qoff:512], start=False, stop=True),
                        [("kpe", kb), ("qpe", hd)], [("pS", sbk)])
                    sc.op("act", lambda: nc.scalar.activation(PTa[sbk][:, 0:N], pS[sbk][:, 0:N], AF.Exp),
                          [("pS", sbk)], [("PTa", sbk)])
                    if kt >= 4 * tb:
                        sc.op("dve", lambda: nc.vector.memset(PTa[sbk][64:128, 0:64], 0.0), [], [("PTa", sbk)])

                def emit_pv(hd, kt, idx):
                    j = max(0, kt - 4 * tb)
                    qoff = j * 128
                    sbk = idx % 4
                    kb = kt // 4
                    for nt in range(j, 4):
                        acc, akey = ACC[nt]
                        sc.op("pe", lambda nt=nt, acc=acc: nc.tensor.matmul(
                            acc[:, 0:130], PTa[sbk][:, nt * 128 - qoff:nt * 128 - qoff + 128], vaug[:, kt, hd, 0:130],
                            start=(kt == 0), stop=(kt == 4 * tb + nt)),
                            [("PTa", sbk), ("vaug", kb)], [akey])
                    if kt == nkt - 1:
                        for nt in range(4):
                            acc, akey = ACC[nt]
                            sc.op("dve", lambda nt=nt, acc=acc: nc.vector.reciprocal(rl[:, hd * 4 + nt:hd * 4 + nt + 1], acc[:, 128:129]),
                                  [akey], [("rl", hd * 4 + nt)])
                            sc.op("dve", lambda nt=nt, acc=acc: nc.vector.tensor_scalar(oout[:, nt, hd * 128:(hd + 1) * 128], acc[:, 0:128],
                                                                                      rl[:, hd * 4 + nt:hd * 4 + nt + 1], None, ALU.mult),
                                  [akey, ("rl", hd * 4 + nt)], ["oout"])

                for i in range(len(tiles) + LA):
                    if i < len(tiles):
                        emit_scores(tiles[i][0], tiles[i][1], cntS[0] + i)
                    if i >= LA:
                        emit_pv(tiles[i - LA][0], tiles[i - LA][1], cntS[0] + i - LA)
                cntS[0] += len(tiles)
                sc.dma("sp", lambda tb=tb: nc.sync.dma_start(
                    out=mix_src[tb][:, 512:768].rearrange("(n p) c -> p n c", p=128), in_=oout[:]),
                    ("mix_src", tb, 1), ["oout"], semkey="mix_src1")
                sc.dma("pool", lambda tb=tb: nc.gpsimd.collective_compute("AllGather", ALU.bypass, replica_groups=RG,
                                                                          ins=[mix_src[tb].opt()], outs=[mix_all[tb].opt()]),
                       ("mix_all", tb), [("mix_src", tb, 0), ("mix_src", tb, 1)], inc=1, semkey="mix_all")
                sc.dma("sp", lambda tb=tb: nc.sync.dma_start(out=mix_comb[tb * 2048:(tb + 1) * 2048, :], in_=mix_all[tb]),
                       "mix_comb", [("mix_all", tb)])
            sc.flush()

        if stop_after == "C2":
            return nc
        with ExitStack() as st:
            mixTs = [sb("mixTs%d" % i, [128, 512], BF16, st) for i in range(2)]
            wro = sb("wro", [128, 16, D], BF16, st)
            wmo = sb("wmo", [128, 8, D], BF16, st)
            wg = sb("wg", [128, 8, 2048], BF16, st)
            mixin = [sb("mixin%d" % i, [128, 4, 768], BF16, st) for i in range(2)]
            ygT = sb("ygT", [128, 24, 512], BF16, st)
            h1T = [sb("h1T%d" % i, [128, 8, 512], BF16, st) for i in range(2)]
            sg = [sb("sg%d" % i, [128, 512], F32, st) for i in range(4)]
            tpp = [ps("tpp%d" % i, [128, 8, 128], BF16, st) for i in range(2)]
            pR = ps("pR", [128, 512], F32, st)
            pM = ps("pM", [128, 512], F32, st)
            pG1 = ps("pG1", [128, 512], F32, st)
            pG2 = ps("pG2", [128, 512], F32, st)
            sc.dma("pool", lambda: nc.gpsimd.dma_start(out=wro[:], in_=wro_d), "wro")
            sc.dma("pool", lambda: nc.gpsimd.dma_start(out=wmo[:], in_=wmo_d), "wmo")
            sc.dma("pool", lambda: nc.gpsimd.dma_start(out=wg[:], in_=wgate_d), "wg")
            def gather_local(j):
                rank = nc.gpsimd.partition_id() % 4
                return nc.gpsimd.dma_start(out=mix_loc[:, j * 512:(j + 1) * 512, :],
                                           in_=mix_comb[bass.ds((rank * NG + j) * 2048, 2048), :].rearrange("(r p) c -> r p c", r=4))

            for j in range(NG):
                sc.dma("pool", lambda j=j: gather_local(j), "mix_loc", ["mix_comb"])
            cnt = [0]
            mcnt = [0]
            for g in range(NG):
                sc.dma("sp", lambda g=g: nc.sync.dma_start(
                    out=h1T[g % 2][:], in_=h1T_src[g].rearrange("(kc p) t -> p kc t", p=128)), ("h1T", g % 2))
                for tt in range(4):
                    t = g * 4 + tt
                    mb = cnt[0] % 2
                    cnt[0] += 1
                    sc.dma("sp", lambda t=t, mb=mb: nc.sync.dma_start(
                        out=mixin[mb][:], in_=mix_loc[:, t * 128:(t + 1) * 128, :].rearrange("r p c -> p r c")), ("mixin", mb), ["mix_loc"])
                    if KDBG in (21, 22, 23, 24):
                        sc.dma("pool", lambda t=t, mb=mb: nc.gpsimd.dma_start(out=out_d[t * 128:(t + 1) * 128, 0:768], in_=mixin[mb][:, KDBG - 21, :]),
                               ("out", t), [("mixin", mb)], semkey="outst")
                    for rr in range(4):
                        for grp in range(2):
                            tb_ = (rr * 2 + grp) % 2
                            ncol = 4 if grp == 0 else 2
                            for jj in range(ncol):
                                col = (jj if grp == 0 else 4 + jj) * 128
                                sc.op("pe", lambda mb=mb, rr=rr, tb_=tb_, jj=jj, col=col: nc.tensor.transpose(
                                    tpp[tb_][:, jj, :], mixin[mb][:, rr, col:col + 128], ident[:]), [("mixin", mb), "ident"], [("tpp", tb_)])
                            ch0 = rr * 4 if grp == 0 else 16 + rr * 2
                            sc.op("dve" if grp == 0 else "act",
                                  (lambda tb_=tb_, ch0=ch0, ncol=ncol, tt=tt: nc.vector.tensor_copy(ygT[:, ch0:ch0 + ncol, tt * 128:(tt + 1) * 128], tpp[tb_][:, 0:ncol, :]))
                                  if grp == 0 else
                                  (lambda tb_=tb_, ch0=ch0, ncol=ncol, tt=tt: nc.scalar.copy(ygT[:, ch0:ch0 + ncol, tt * 128:(tt + 1) * 128], tpp[tb_][:, 0:ncol, :])),
                                  [("tpp", tb_)], [("ygT", tt)])
                ygk = [("ygT", tt) for tt in range(4)]
                hk = ("h1T", g % 2)
                hT_ = h1T[g % 2]
                for dc in range(8):
                    for kc in range(8):
                        sc.op("pe", lambda dc=dc, kc=kc, hT_=hT_: nc.tensor.matmul(pG1[:], wg[:, kc, dc * 128:(dc + 1) * 128], hT_[:, kc, :], start=(kc == 0), stop=(kc == 7)),
                              ["wg", hk], ["pG1"])
                    for kc in range(8):
                        sc.op("pe", lambda dc=dc, kc=kc, hT_=hT_: nc.tensor.matmul(pG2[:], wg[:, kc, 1024 + dc * 128:1024 + (dc + 1) * 128], hT_[:, kc, :], start=(kc == 0), stop=(kc == 7)),
                              ["wg", hk], ["pG2"])
                    for ch in range(16):
                        sc.op("pe", lambda dc=dc, ch=ch: nc.tensor.matmul(pR[:], wro[:, ch, dc * 128:(dc + 1) * 128], ygT[:, ch, :], start=(ch == 0), stop=(ch == 15)),
                              ["wro"] + ygk, ["pR"])
                    for ch in range(8):
                        sc.op("pe", lambda dc=dc, ch=ch: nc.tensor.matmul(pM[:], wmo[:, ch, dc * 128:(dc + 1) * 128], ygT[:, 16 + ch, :], start=(ch == 0), stop=(ch == 7)),
                              ["wmo"] + ygk, ["pM"])
                    sc.op("act", lambda: nc.scalar.activation(sg[0][:], pG1[:], AF.Sigmoid), ["pG1"], [("sg", 0)])
                    sc.op("act", lambda: nc.scalar.activation(sg[1][:], pG2[:], AF.Sigmoid), ["pG2"], [("sg", 1)])
                    sc.op("dve", lambda: nc.vector.tensor_tensor(sg[2][:], sg[0][:], pR[:], ALU.mult), [("sg", 0), "pR"], [("sg", 2)])
                    sc.op("dve", lambda: nc.vector.tensor_tensor(sg[3][:], sg[1][:], pM[:], ALU.mult), [("sg", 1), "pM"], [("sg", 3)])
                    ms = mcnt[0] % 2
                    mcnt[0] += 1
                    sc.op("pool", lambda ms=ms: nc.gpsimd.tensor_tensor(mixTs[ms][:], sg[2][:], sg[3][:], ALU.add),
                          [("sg", 2), ("sg", 3)], [("mixTs", ms)])
                    sc.dma("sp", lambda ms=ms, dc=dc, g=g: nc.sync.dma_start(out=mixT_d[dc * 128:(dc + 1) * 128, g * 512:(g + 1) * 512], in_=mixTs[ms][:]),
                           "mixT_d", [("mixTs", ms)])
            sc.flush()

        hst = ExitStack()
        H["h"] = h = sb("h", [128, NT, D], F32, hst)
        with ExitStack() as st:
            mixT = sb("mixT", [128, 8, T], BF16, st)
            for dc in range(8):
                sc.dma("sp", lambda dc=dc: nc.sync.dma_start(out=mixT[:, dc, :], in_=mixT_d[dc * 128:(dc + 1) * 128, :]), "mixT", ["mixT_d"])
            wo = sb("wo", [128, 8, D], BF16, st)
            lng = sb("lng", [128, D], F32, st)
            lnb = sb("lnb", [128, D], F32, st)
            junk = sb("junk", [128, D], F32, st)
            stat = sb("stat", [128, 8, NT], F32, st)
            po = [ps("po%d" % i, [128, 512], F32, st) for i in range(2)]
            sc.dma("pool", lambda: nc.gpsimd.dma_start(out=wo[:], in_=wout_d), "wo")
            sc.dma("sp", lambda: nc.sync.dma_start(out=lng[:], in_=lnp_d[1]), "lng")
            sc.dma("sp", lambda: nc.sync.dma_start(out=lnb[:], in_=lnp_d[5]), "lnb")
            for t in range(NT):
                sc.dma("sp", lambda t=t: nc.sync.dma_start(out=h[:, t, :], in_=h_spill[t * 128:(t + 1) * 128, :]), ("h", t), semkey=("hload", t // 4))
            c2 = 0
            for t in range(NT):
                for half in range(2):
                    b = c2 % 2
                    c2 += 1
                    for dc in range(8):
                        sc.op("pe", lambda b=b, dc=dc, t=t, half=half: nc.tensor.matmul(
                            po[b][:], mixT[:, dc, t * 128:(t + 1) * 128], wo[:, dc, half * 512:(half + 1) * 512], start=(dc == 0), stop=(dc == 7)),
                            ["wo", "mixT"], [("po", b)])
                    hs = h[:, t, half * 512:(half + 1) * 512]
                    if KDBG == 20:
                        sc.op("dve", lambda b=b, hs=hs: nc.vector.tensor_copy(hs, po[b][:]), [("po", b), ("h", t)], [("h", t)])
                    else:
                        sc.op("dve", lambda b=b, hs=hs: nc.vector.scalar_tensor_tensor(hs, hs, ALPHA, po[b][:], ALU.mult, ALU.add),
                              [("po", b), ("h", t)], [("h", t)])
            if KDBG != 20:
                layer_norm(st, range(NT), 1, lng, lnb, junk, stat)
            sc.flush()

        if stop_after == "D":
            dump_h()
            hst.close()
            return nc

        ffn_phase(1, 2)

        with ExitStack() as st:
            xT = sb("xT", [128, 8, T], BF16, st)
            hb = [sb("hb%d" % i, [128, D], BF16, st) for i in range(2)]
            wpg = sb("wpg", [128, 8, D], BF16, st)
            wpp = sb("wpp", [128, 2, D], BF16, st)
            pb = sb("pb", [128, NT, 256], BF16, st)
            pT = sb("pT", [128, 2, T], BF16, st)
            sgp = [sb("sgp%d" % i, [128, 512], F32, st) for i in range(2)]
            lng = sb("lng", [128, D], F32, st)
            lnb = sb("lnb", [128, D], F32, st)
            junk = sb("junk", [128, D], F32, st)
            stat = sb("stat", [128, 8, NT], F32, st)
            tp = [ps("tp%d" % i, [128, 8, 128], BF16, st) for i in range(2)]
            pG = [ps("pG%d" % i, [128, 512], F32, st) for i in range(2)]
            pP = [ps("pP%d" % i, [128, 512], F32, st) for i in range(2)]
            sc.dma("pool", lambda: nc.gpsimd.dma_start(out=wpg[:], in_=wpg_d), "wpg")
            sc.dma("pool", lambda: nc.gpsimd.dma_start(out=wpp[:], in_=wpp_d), "wpp")
            sc.dma("pool", lambda: nc.gpsimd.dma_start(out=pb[:], in_=p_d.rearrange("(n p) c -> p n c", p=128)), "pb")
            sc.dma("sp", lambda: nc.sync.dma_start(out=lng[:], in_=lnp_d[3]), "lng")
            sc.dma("sp", lambda: nc.sync.dma_start(out=lnb[:], in_=lnp_d[7]), "lnb")
            make_xT(st, xT, hb, tp, True)
            for t in range(NT):
                b = t % 2
                for c in range(2):
                    sc.op("pe", lambda b=b, t=t, c=c: nc.tensor.transpose(tp[b][:, c, :], pb[:, t, c * 128:(c + 1) * 128], ident[:]),
                          ["pb", "ident"], [("tp", b)])
                sc.op("dve", lambda b=b, t=t: nc.vector.tensor_copy(pT[:, :, t * 128:(t + 1) * 128], tp[b][:, 0:2, :]), [("tp", b)], [("pT", t)])
            c3 = 0
            for t in range(NT):
                for half in range(2):
                    b = c3 % 2
                    c3 += 1
                    for kc in range(8):
                        sc.op("pe", lambda b=b, kc=kc, t=t, half=half: nc.tensor.matmul(
                            pG[b][:], xT[:, kc, t * 128:(t + 1) * 128], wpg[:, kc, half * 512:(half + 1) * 512], start=(kc == 0), stop=(kc == 7)),
                            ["wpg", ("xT", t // 4)], [("pG", b)])
                    for c in range(2):
                        sc.op("pe", lambda b=b, c=c, t=t, half=half: nc.tensor.matmul(
                            pP[b][:], pT[:, c, t * 128:(t + 1) * 128], wpp[:, c, half * 512:(half + 1) * 512], start=(c == 0), stop=(c == 1)),
                            ["wpp", ("pT", t)], [("pP", b)])
                    sc.op("act", lambda b=b: nc.scalar.activation(sgp[b][:], pG[b][:], AF.Sigmoid), [("pG", b)], [("sgp", b)])
                    sc.op("dve", lambda b=b: nc.vector.tensor_tensor(sgp[b][:], sgp[b][:], pP[b][:], ALU.mult), [("sgp", b), ("pP", b)], [("sgp", b)])
                    hs = h[:, t, half * 512:(half + 1) * 512]
                    sc.op("pool", lambda b=b, hs=hs: nc.gpsimd.tensor_tensor(hs, hs, sgp[b][:], ALU.add), [("sgp", b), ("h", t)], [("h", t)])

            def store(t):
                sc.dma("sp", lambda t=t: nc.sync.dma_start(out=out_d[t * 128:(t + 1) * 128, :], in_=h[:, t, :]), ("out", t), [("h", t)], semkey="outst")

            layer_norm(st, range(NT), 3, lng, lnb, junk, stat, store)
            sc.flush()
        hst.close()
    return nc


def _kmajor(w):
    K, N = w.shape
    return np.ascontiguousarray(w.reshape(K // 128, 128, N).transpose(1, 0, 2))


def prep_inputs(inp, S):
    T = S // 4
    f = np.float32
    ln_g = np.asarray(inp["ln_g"], f)[0]
    ln_b = np.asarray(inp["ln_b"], f)[0]
    lnp = np.ascontiguousarray(np.broadcast_to(np.concatenate([ln_g, ln_b], 0)[:, None, :], (8, 128, D)))

    def w1_layout(w):
        w = np.asarray(w, f)[0]
        perm = []
        for c0, c1 in PIECES:
            perm += list(range(c0 * 128, c1 * 128)) + list(range(DFF + c0 * 128, DFF + c1 * 128))
        return _kmajor(w[:, perm])

    w_in = np.asarray(inp["w_in"], f)[0]
    w_uq = np.asarray(inp["w_uq"], f)[0]
    w_ukv = np.asarray(inp["w_ukv"], f)[0]
    gn = np.asarray(inp["ret_gn_g"], f)[0]
    cst = np.zeros((128, 8), f)
    d = np.arange(128)
    cst[:, 0] = (10000.0 ** (-(d % 64).astype(np.float64) / 64.0)).astype(f)
    cst[:, 1] = (10000.0 ** (-(d % 32).astype(np.float64) / 32.0)).astype(f)
    sgn_r = np.where(d < 64, -1.0, 1.0)
    sgn_m = np.where((d % 64) < 32, -1.0, 1.0)
    cst[:, 2] = sgn_r
    cst[:, 3] = sgn_r * PI_SAFE
    cst[:, 4] = sgn_m
    cst[:, 5] = sgn_m * PI_SAFE
    cst[:, 6] = PI_SAFE
    common = {
        "lnp": lnp,
        "ffn1_w1": w1_layout(inp["ffn1_w_in"]),
        "ffn2_w1": w1_layout(inp["ffn2_w_in"]),
        "ffn1_w2": _kmajor(np.asarray(inp["ffn1_w_out"], f)[0]),
        "ffn2_w2": _kmajor(np.asarray(inp["ffn2_w_out"], f)[0]),
        "ident": np.eye(128, dtype=f),
        "wgate": _kmajor(w_in[:, 6720:8768]),
        "ng": np.ascontiguousarray(np.concatenate([np.asarray(inp["q_norm_g"], f)[0].reshape(2, 128).T,
                                                   np.asarray(inp["kv_norm_g"], f)[0].reshape(2, 128).T], 1)),
        "wro": _kmajor(np.asarray(inp["w_ret_o"], f)[0]),
        "wmo": _kmajor(np.asarray(inp["w_mla_o"], f)[0]),
        "wout": _kmajor(np.asarray(inp["w_out"], f)[0]),
        "wpg": _kmajor(np.asarray(inp["ple_w_gate"], f)[0]),
        "wpp": _kmajor(np.asarray(inp["ple_w_proj"], f)[0]),
        "cst": cst,
    }
    x = np.asarray(inp["x"], f)
    p = np.asarray(inp["p"], f)[0]
    pos = np.asarray(inp["positions"], np.int32)
    rot128 = np.concatenate([np.arange(64, 128), np.arange(0, 64)])
    rot64 = np.concatenate([np.arange(32, 64), np.arange(0, 32)])
    n = np.arange(512)
    maps = []
    for core in range(8):
        b, r = core // 4, core % 4
        heads = (2 * r, 2 * r + 1)
        m = dict(common)
        m["x"] = np.ascontiguousarray(x[b, r * T:(r + 1) * T])
        m["p"] = np.ascontiguousarray(p[b, r * T:(r + 1) * T])
        m["pos"] = np.ascontiguousarray(np.broadcast_to(pos[b][None, :], (128, S)))
        cols = []
        for base in (0, 1024):
            for hh in heads:
                cols.append(base + hh * 128 + np.arange(128))
            for hh in heads:
                cols.append(base + hh * 128 + rot128)
        cols.append(6144 + np.arange(256))
        cols.append(6400 + np.arange(256))
        cols.append(6656 + np.arange(64))
        cols.append(6656 + rot64)
        for base in (2048, 4096):
            for hh in heads:
                cols.append(base + hh * 256 + np.arange(256))
        cols = np.concatenate(cols)
        assert cols.shape[0] == MIXW
        m["wmix"] = _kmajor(w_in[:, cols])
        cq = []
        for hh in heads:
            cq.append(hh * 192 + np.arange(128))
            cq.append(hh * 192 + 128 + np.arange(64))
            cq.append(hh * 192 + 128 + rot64)
        m["wuq"] = _kmajor(w_uq[:, np.concatenate(cq)])
        ckv = [hh * 256 + np.arange(128) for hh in heads] + [hh * 256 + 128 + np.arange(128) for hh in heads]
        m["wukv"] = _kmajor(w_ukv[:, np.concatenate(ckv)])
        m["gng"] = np.ascontiguousarray(np.broadcast_to(
            np.concatenate([gn[hh * 256:(hh + 1) * 256] for hh in heads])[None, :], (128, 512)))
        dtab = np.zeros((2, 128, 4, 512), f)
        xit = np.zeros((2, 128, 512), f)
        zeta = np.zeros((128, 8), f)
        for j, hh in enumerate(heads):
            lg = math.log(1.0 - 2.0 ** (-5.0 - hh))
            for kt in range(4):
                mm = kt * 128 + np.arange(128)
                dist = np.abs(n[None, :] - mm[:, None]).astype(np.float64)
                ok = (mm[:, None] // 64) <= (n[None, :] // 64)
                dtab[j, :, kt, :] = np.where(ok, np.exp(lg * dist) * (128.0 ** -0.5), 0.0)
                zeta[:, j * 4 + kt] = np.exp(lg * (511.0 - mm)) * (128.0 ** -0.5)
            xit[j, :, :] = np.exp(lg * (n + 1.0))[None, :]
        m["dtab"] = dtab
        m["xitab"] = xit
        m["zeta"] = zeta
        maps.append(m)
    return maps


_PROG = {}


def run(inputs, S, stop_after=None):
    key = (S, stop_after)
    if key not in _PROG:
        _PROG[key] = build_program(S, stop_after)
    nc = _PROG[key]
    maps = prep_inputs(inputs, S)
    res = run_bass_kernel_spmd(nc, maps, core_ids=list(range(8)))
    T = S // 4
    out = np.zeros((2, S, D), np.float32)
    for core in range(8):
        b, r = core // 4, core % 4
        out[b, r * T:(r + 1) * T] = np.asarray(res.results[core]["out"])
    return out


def kernel(**inputs):
    return run(inputs, 8192)
```

```python
import math
import os
KDBG = int(os.environ.get('KDBG', '9'))
from contextlib import ExitStack

import numpy as np
import concourse.bass as bass
import concourse.mybir as mybir
from concourse.bass_utils import run_bass_kernel_spmd

F32 = mybir.dt.float32
BF16 = mybir.dt.bfloat16
I32 = mybir.dt.int32
AF = mybir.ActivationFunctionType
ALU = mybir.AluOpType
AX = mybir.AxisListType

D = 1024
DFF = 2816
NFC = DFF // 128
PIECES = [(0, 4), (4, 8), (8, 12), (12, 16), (16, 19), (19, 22)]
ALPHA = 2.0 ** 0.25
EPS = 1e-5
TWO_PI = 2.0 * math.pi
PI_SAFE = 3.1415925
CW1 = 6.28125
CW2 = TWO_PI - 6.28125
MIXW = 1664 + 1024
ATT_SCALE = 192.0 ** -0.5


class _Op:
    __slots__ = ("eng", "fn", "deps", "need_inc", "inc_value", "dsem", "dinc", "phase")

    def __init__(self, eng, fn, phase):
        self.eng = eng
        self.fn = fn
        self.deps = []
        self.need_inc = False
        self.inc_value = None
        self.dsem = None
        self.dinc = 16
        self.phase = phase


class Sched:
    ENG = ("pe", "act", "dve", "pool", "sp")

    def __init__(self, nc, stack):
        self.nc = nc
        self.stack = stack
        self.esem = {e: stack.enter_context(nc.semaphore("es_" + e)) for e in self.ENG}
        self.bar = stack.enter_context(nc.semaphore("bar"))
        self.ecount = {e: 0 for e in self.ENG}
        self.ops = []
        self.last_write = {}
        self.readers = {}
        self.dsems = {}
        self.phase = 0
        self.waited = {e: {} for e in self.ENG}
        self.nsem = 0

    def _dep(self, op, tok, raw):
        if tok is None:
            return
        if tok[0] == "op":
            src = tok[1]
            if src.phase != self.phase:
                return
            if src.eng == op.eng:
                if op.eng in ("pe", "sp") or not raw:
                    return
            src.need_inc = True
            op.deps.append(("op", src))
        else:
            _, key, phase = tok
            if phase != self.phase:
                return
            sem, total = self.dsems[key]
            op.deps.append(("dma", sem, total))

    def _track(self, op, tok, reads, writes):
        for k in reads:
            self._dep(op, self.last_write.get(k), True)
        for k in writes:
            self._dep(op, self.last_write.get(k), True)
            for r in self.readers.get(k, ()):
                self._dep(op, r, False)
        for k in reads:
            self.readers.setdefault(k, []).append(tok)
        for k in writes:
            self.last_write[k] = tok
            self.readers[k] = []

    def op(self, eng, fn, reads=(), writes=()):
        o = _Op(eng, fn, self.phase)
        self._track(o, ("op", o), reads, writes)
        self.ops.append(o)
        return o

    def dma(self, eng, fn, key, reads=(), inc=16, semkey=None):
        o = _Op(eng, fn, self.phase)
        if semkey is None:
            semkey = key
        if semkey not in self.dsems:
            self.nsem += 1
            self.dsems[semkey] = [self.stack.enter_context(self.nc.semaphore("d%d" % self.nsem)), 0]
        ent = self.dsems[semkey]
        tok = ("dma", semkey, self.phase)
        self._track(o, tok, reads, (key,))
        ent[1] += inc
        o.dsem = ent[0]
        o.dinc = inc
        self.ops.append(o)
        return o

    def flush(self):
        nc = self.nc
        ops = self.ops
        self.ops = []
        last = {}
        for o in ops:
            if o.dsem is None:
                last[o.eng] = o
        for e in ("pe", "act", "dve", "pool"):
            if e in last:
                last[e].need_inc = True
        for o in ops:
            if o.need_inc and o.dsem is None:
                self.ecount[o.eng] += 1
                o.inc_value = self.ecount[o.eng]
            elif o.need_inc:
                raise AssertionError("dma op used as engine token")
        self.phase += 1
        bar_val = self.phase
        final_waits = [(self.esem[e], self.ecount[e]) for e in ("pe", "act", "dve", "pool") if self.ecount[e]]
        final_waits += [(s, t) for (s, t) in self.dsems.values() if t]
        by = {e: [o for o in ops if o.eng == e] for e in self.ENG}
        objs = {"pe": nc.tensor, "act": nc.scalar, "dve": nc.vector, "pool": nc.gpsimd, "sp": nc.sync}

        def body(e):
            eo = objs[e]
            wd = self.waited[e]

            def wait(sem, val):
                if wd.get(id(sem), 0) < val:
                    eo.wait_ge(sem, val)
                    wd[id(sem)] = val

            for o in by[e]:
                for d in o.deps:
                    if d[0] == "op":
                        wait(self.esem[d[1].eng], d[1].inc_value)
                    else:
                        wait(d[1], d[2])
                ins = o.fn()
                if o.dsem is not None:
                    if o.dinc == 16:
                        ins.then_inc(o.dsem, 16)
                    else:
                        ins.then_inc(o.dsem)
                elif o.need_inc:
                    ins.then_inc(self.esem[e], 1)
            if e == "sp":
                for sem, val in final_waits:
                    wait(sem, val)
                eo.nop().then_inc(self.bar, 1)
            else:
                wait(self.bar, bar_val)

        self._simulate(by, final_waits, bar_val)
        with nc.named_scope("ph%d" % self.phase), nc.Block() as block:
            block.tensor(lambda _e: body("pe"))
            block.scalar(lambda _e: body("act"))
            block.vector(lambda _e: body("dve"))
            block.gpsimd(lambda _e: body("pool"))
            block.sync(lambda _e: body("sp"))


def _sched_simulate(self, by, final_waits, bar_val):
    semv = dict(getattr(self, "_semv", {}))
    pc = {e: 0 for e in self.ENG}
    n = {e: len(by[e]) + 1 for e in self.ENG}

    def ready(e, i):
        if i < len(by[e]):
            o = by[e][i]
            for d in o.deps:
                if d[0] == "op":
                    if semv.get(id(self.esem[d[1].eng]), 0) < d[1].inc_value:
                        return False
                elif semv.get(id(d[1]), 0) < d[2]:
                    return False
            return True
        if e == "sp":
            return all(semv.get(id(sm), 0) >= v for sm, v in final_waits)
        return semv.get(id(self.bar), 0) >= bar_val

    def fire(e, i):
        if i < len(by[e]):
            o = by[e][i]
            if o.dsem is not None:
                semv[id(o.dsem)] = semv.get(id(o.dsem), 0) + o.dinc
            elif o.need_inc:
                semv[id(self.esem[e])] = semv.get(id(self.esem[e]), 0) + 1
        elif e == "sp":
            semv[id(self.bar)] = semv.get(id(self.bar), 0) + 1

    progress = True
    while progress:
        progress = False
        for e in self.ENG:
            while pc[e] < n[e] and ready(e, pc[e]):
                fire(e, pc[e])
                pc[e] += 1
                progress = True
    stuck = {e: (pc[e], n[e]) for e in self.ENG if pc[e] < n[e]}
    if stuck:
        raise AssertionError("scheduler deadlock: %r" % (stuck,))
    self._semv = semv


Sched._simulate = _sched_simulate


def build_program(S, stop_after=None):
    T = S // 4
    NT = T // 128
    NG = T // 512
    NB = S // 512
    nc = bass.Bass("TRN2", target_bir_lowering=False)

    def din(name, shape, dt=F32):
        return nc.dram_tensor(name, list(shape), dt, kind="ExternalInput").ap()

    x_d = din("x", [T, D])
    p_d = din("p", [T, 256])
    pos_d = din("pos", [128, S], I32)
    lnp_d = din("lnp", [8, 128, D])
    w1_d = [din("ffn1_w1", [128, 8, 2 * DFF]), din("ffn2_w1", [128, 8, 2 * DFF])]
    w2_d = [din("ffn1_w2", [128, NFC, D]), din("ffn2_w2", [128, NFC, D])]
    wmix_d = din("wmix", [128, 8, MIXW])
    wgate_d = din("wgate", [128, 8, 2048])
    wuq_d = din("wuq", [128, 2, 512])
    wukv_d = din("wukv", [128, 2, 512])
    ng_d = din("ng", [128, 4])
    gng_d = din("gng", [128, 512])
    wro_d = din("wro", [128, 16, D])
    wmo_d = din("wmo", [128, 8, D])
    wout_d = din("wout", [128, 8, D])
    wpg_d = din("wpg", [128, 8, D])
    wpp_d = din("wpp", [128, 2, D])
    ident_d = din("ident", [128, 128])
    cst_d = din("cst", [128, 8])
    dtab_d = din("dtab", [2, 128, 4, 512])
    xitab_d = din("xitab", [2, 128, 512])
    zeta_d = din("zeta", [128, 8])
    out_d = nc.dram_tensor("out", [T, D], F32, kind="ExternalOutput").ap()

    h1T_src = [nc.dram_tensor("h1T_src%d" % g, [D, 512], BF16, kind="Internal").ap() for g in range(NG)]
    h1T_all = [nc.dram_tensor("h1T_all%d" % g, [4 * D, 512], BF16, kind="Internal").ap() for g in range(NG)]
    h_spill = nc.dram_tensor("h_spill", [T, D], F32, kind="Internal").ap()
    mix_src = [nc.dram_tensor("mix_src%d" % i, [512, 768], BF16, kind="Internal").ap() for i in range(NB)]
    mix_all = [nc.dram_tensor("mix_all%d" % i, [2048, 768], BF16, kind="Internal").ap() for i in range(NB)]
    mix_comb = nc.dram_tensor("mix_comb", [NB * 2048, 768], BF16, kind="Internal").ap()
    mixT_d = nc.dram_tensor("mixT_d", [D, T], BF16, kind="Internal").ap()
    mix_loc = nc.dram_tensor("mix_loc", [4, T, 768], BF16, kind="Internal").ap()
    RG = [[0, 1, 2, 3], [4, 5, 6, 7]]

    with ExitStack() as top:
        sc = Sched(nc, top)

        uid = [0]

        def sb(name, shape, dt, stack=top):
            uid[0] += 1
            return stack.enter_context(nc.sbuf_tensor("sb%d_%s" % (uid[0], name), list(shape), dt))

        def ps(name, shape, dt, stack):
            uid[0] += 1
            return stack.enter_context(nc.psum_tensor("ps%d_%s" % (uid[0], name), list(shape), dt))

        ident = sb("ident", [128, 128], BF16)
        sc.dma("pool", lambda: nc.gpsimd.dma_start(out=ident[:], in_=ident_d), "ident")
        H = {}
        hst = ExitStack()
        H["h"] = sb("h", [128, NT, D], F32, hst)

        def layer_norm(st, tiles, ln_idx, lng, lnb, junk, stat, out_fn=None):
            tiles = list(tiles)
            h = H["h"]
            ks = "lnstat"
            for t in tiles:
                ht = h[:, t, :]
                k = ("h", t)
                sc.op("dve", lambda ht=ht, t=t: nc.vector.reduce_sum(stat[:, 0, t:t + 1], ht, AX.X), [k], [("s1", t)])
                sc.op("act", lambda ht=ht, t=t: nc.scalar.activation(junk[:], ht, AF.Square, accum_out=stat[:, 1, t:t + 1]),
                      [k], [("s2", t), "lnjunk"])
            t0, t1 = tiles[0], tiles[-1] + 1
            S1, S2, MEAN, VAR, RSTD, NB = [stat[:, j, t0:t1] for j in range(6)]
            rd = [("s1", t) for t in tiles] + [("s2", t) for t in tiles]
            sc.op("dve", lambda: nc.vector.tensor_scalar(MEAN, S1, 1.0 / D, None, ALU.mult), rd, [ks])
            sc.op("dve", lambda: nc.vector.tensor_tensor(VAR, MEAN, MEAN, ALU.mult), [ks], [ks])
            sc.op("dve", lambda: nc.vector.scalar_tensor_tensor(VAR, S2, 1.0 / D, VAR, ALU.mult, ALU.subtract), [ks] + rd, [ks])
            sc.op("dve", lambda: nc.vector.tensor_scalar(VAR, VAR, EPS, None, ALU.add), [ks], [ks])
            sc.op("act", lambda: nc.scalar.sqrt(RSTD, VAR), [ks], [ks])
            sc.op("dve", lambda: nc.vector.reciprocal(RSTD, RSTD), [ks], [ks])
            sc.op("dve", lambda: nc.vector.scalar_tensor_tensor(NB, MEAN, -1.0, RSTD, ALU.mult, ALU.mult), [ks], [ks])
            for t in tiles:
                ht = h[:, t, :]
                k = ("h", t)
                sc.op("act", lambda ht=ht, t=t: nc.scalar.activation(ht, ht, AF.Identity, bias=stat[:, 5, t:t + 1], scale=stat[:, 4, t:t + 1]),
                      [k, ks], [k])
                sc.op("dve", lambda ht=ht: nc.vector.tensor_tensor(ht, ht, lng[:], ALU.mult), [k, "lng"], [k])
                sc.op("pool", lambda ht=ht: nc.gpsimd.tensor_tensor(ht, ht, lnb[:], ALU.add), [k, "lnb"], [k])
                if out_fn is not None:
                    out_fn(t)

        def make_xT(st, xT, hb, tp, scale_after):
            h = H["h"]
            for t in range(NT):
                b = t % 2
                k = ("h", t)
                sc.op("act", lambda t=t, b=b: nc.scalar.copy(hb[b][:], h[:, t, :]), [k], [("hb", b)])
                for kc in range(8):
                    sc.op("pe", lambda b=b, kc=kc: nc.tensor.transpose(tp[b][:, kc, :], hb[b][:, kc * 128:(kc + 1) * 128], ident[:]),
                          [("hb", b), "ident"], [("tp", b)])
                sc.op("dve", lambda t=t, b=b: nc.vector.tensor_copy(xT[:, :, t * 128:(t + 1) * 128], tp[b][:]),
                      [("tp", b)], [("xT", t // 4)])
                if scale_after:
                    sc.op("pool", lambda t=t: nc.gpsimd.tensor_scalar(h[:, t, :], h[:, t, :], ALPHA, None, ALU.mult), [k], [k])

        def ffn_phase(fi, ln_idx, after_tile=None):
            h = H["h"]
            with ExitStack() as st:
                xT = sb("xT", [128, 8, T], BF16, st)
                hb = [sb("hb%d" % i, [128, D], BF16, st) for i in range(2)]
                w1s = [sb("w1s%d" % i, [128, 8, 1024], BF16, st) for i in range(2)]
                w2s = [sb("w2s%d" % i, [128, 4, D], BF16, st) for i in range(2)]
                hT = [sb("hT%d" % i, [128, 4, 512], BF16, st) for i in range(2)]
                sgs = [sb("sgs%d" % i, [128, 512], F32, st) for i in range(2)]
                lng = sb("lng", [128, D], F32, st)
                lnb = sb("lnb", [128, D], F32, st)
                junk = sb("junk", [128, D], F32, st)
                stat = sb("stat", [128, 8, NT], F32, st)
                tp = [ps("tp%d" % i, [128, 8, 128], BF16, st) for i in range(2)]
                pg = [ps("pg%d" % i, [128, 512], F32, st) for i in range(2)]
                po = [ps("po%d" % i, [128, 512], F32, st) for i in range(4)]
                sc.dma("sp", lambda: nc.sync.dma_start(out=lng[:], in_=lnp_d[ln_idx]), "lng")
                sc.dma("sp", lambda: nc.sync.dma_start(out=lnb[:], in_=lnp_d[4 + ln_idx]), "lnb")

                def load_piece(pi):
                    c0, c1 = PIECES[pi]
                    P = c1 - c0
                    slot = pi % 2
                    sc.dma("pool", lambda: nc.gpsimd.dma_start(out=w1s[slot][:, :, 0:2 * P * 128],
                                                               in_=w1_d[fi][:, :, 2 * c0 * 128:2 * c1 * 128]), ("w1s", slot))
                    sc.dma("pool", lambda: nc.gpsimd.dma_start(out=w2s[slot][:, 0:P, :], in_=w2_d[fi][:, c0:c1, :]), ("w2s", slot))

                load_piece(0)
                make_xT(st, xT, hb, tp, True)
                steps = [(pi, g) for pi in range(len(PIECES)) for g in range(NG)]
                cnt = [0, 0]

                def gu(si):
                    pi, g = steps[si]
                    c0, c1 = PIECES[pi]
                    P = c1 - c0
                    slot = pi % 2
                    hbuf = si % 2
                    for ci in range(P):
                        for which in range(2):
                            b = cnt[0] % 2
                            cnt[0] += 1
                            col = (which * P + ci) * 128
                            for kc in range(8):
                                sc.op("pe", lambda b=b, kc=kc, col=col, slot=slot, g=g: nc.tensor.matmul(
                                    pg[b][:], w1s[slot][:, kc, col:col + 128], xT[:, kc, g * 512:(g + 1) * 512],
                                    start=(kc == 0), stop=(kc == 7)),
                                    [("w1s", slot), ("xT", g)], [("pg", b)])
                            if which == 0:
                                sb_ = ci % 2
                                sc.op("act", lambda b=b, sb_=sb_: nc.scalar.activation(sgs[sb_][:], pg[b][:], AF.Silu),
                                      [("pg", b)], [("sgs", sb_)])
                            else:
                                sc.op("dve", lambda b=b, sb_=sb_, ci=ci, hbuf=hbuf: nc.vector.tensor_tensor(
                                    hT[hbuf][:, ci, :], sgs[sb_][:], pg[b][:], ALU.mult),
                                    [("pg", b), ("sgs", sb_)], [("hT", hbuf)])

                def outp(si):
                    pi, g = steps[si]
                    c0, c1 = PIECES[pi]
                    P = c1 - c0
                    slot = pi % 2
                    hbuf = si % 2
                    for tt in range(4):
                        t = g * 4 + tt
                        for half in range(2):
                            b = cnt[1] % 4
                            cnt[1] += 1
                            for ci in range(P):
                                sc.op("pe", lambda b=b, ci=ci, tt=tt, half=half, slot=slot, hbuf=hbuf: nc.tensor.matmul(
                                    po[b][:], hT[hbuf][:, ci, tt * 128:(tt + 1) * 128], w2s[slot][:, ci, half * 512:(half + 1) * 512],
                                    start=(ci == 0), stop=(ci == P - 1)),
                                    [("hT", hbuf), ("w2s", slot)], [("po", b)])
                            hs = h[:, t, half * 512:(half + 1) * 512]
                            eng = "dve" if half == 0 else "pool"
                            if eng == "dve":
                                sc.op("dve", lambda b=b, hs=hs: nc.vector.scalar_tensor_tensor(hs, po[b][:], 0.5, hs, ALU.mult, ALU.add),
                                      [("po", b), ("h", t)], [("h", t)])
                            else:
                                sc.op("dve", lambda b=b, hs=hs: nc.vector.scalar_tensor_tensor(hs, po[b][:], 0.5, hs, ALU.mult, ALU.add),
                                      [("po", b), ("h", t)], [("h", t)])

                for si in range(len(steps) + 1):
                    if si < len(steps):
                        gu(si)
                    if si >= 1:
                        outp(si - 1)
                    if si < len(steps) and steps[si][1] == 0 and steps[si][0] + 1 < len(PIECES):
                        load_piece(steps[si][0] + 1)
                layer_norm(st, range(NT), ln_idx, lng, lnb, junk, stat, after_tile)
                sc.flush()

        h = H["h"]
        for t in range(NT):
            sc.dma("sp", lambda t=t, h=h: nc.sync.dma_start(out=h[:, t, :], in_=x_d[t * 128:(t + 1) * 128, :]), ("h", t), semkey=("hload", t // 4))
        ffn_phase(0, 0)

        def dump_h():
            h = H["h"]
            if KDBG in (21, 22, 23, 24):
                return
            for t in range(NT):
                sc.dma("sp", lambda t=t: nc.sync.dma_start(out=out_d[t * 128:(t + 1) * 128, :], in_=h[:, t, :]), ("out", t), [("h", t)], semkey="outst")
            sc.flush()

        if stop_after == "A":
            dump_h()
            hst.close()
            return nc

        with ExitStack() as st:
            xT = sb("xT", [128, 8, T], BF16, st)
            hb = [sb("hb%d" % i, [128, D], BF16, st) for i in range(2)]
            tp = [ps("tp%d" % i, [128, 8, 128], BF16, st) for i in range(2)]
            for t in range(NT):
                sc.dma("sp", lambda t=t: nc.sync.dma_start(out=h_spill[t * 128:(t + 1) * 128, :], in_=h[:, t, :]), ("hsp", t), [("h", t)], semkey="hsp")
            make_xT(st, xT, hb, tp, False)
            for g in range(NG):
                for kc in range(8):
                    sc.dma("sp", lambda kc=kc, g=g: nc.sync.dma_start(out=h1T_src[g][kc * 128:(kc + 1) * 128, :], in_=xT[:, kc, g * 512:(g + 1) * 512]),
                           ("h1T_src", g), [("xT", g)])
                sc.dma("pool", lambda g=g: nc.gpsimd.collective_compute("AllGather", ALU.bypass, replica_groups=RG,
                                                                        ins=[h1T_src[g].opt()], outs=[h1T_all[g].opt()]),
                       ("h1T_all", g), [("h1T_src", g)], inc=1, semkey="h1T_all")
            sc.flush()
        if stop_after == "B":
            dump_h()
            hst.close()
            return nc
        hst.close()

        def load_hblk(hblk, tb):
            rr, j = tb // NG, tb % NG
            src = h1T_all[j][rr * D:(rr + 1) * D, :].rearrange("(kc p) t -> p kc t", p=128)
            sc.dma("sp", lambda: nc.sync.dma_start(out=hblk[tb % 2][:], in_=src), ("hblk", tb % 2), [("h1T_all", j)])

        def rope_tables(tb, posi, posf, ang, r1, cosT, sinT, cst, col, kcos, ksin, ki):
            sc.dma("sp", lambda: nc.sync.dma_start(out=posi[:], in_=pos_d[:, tb * 512:(tb + 1) * 512]), "posi")
            sc.op("dve", lambda: nc.vector.tensor_copy(posf[:], posi[:]), ["posi"], ["posf"])
            sc.op("dve", lambda: nc.vector.tensor_scalar(ang[:], posf[:], cst[:, col:col + 1], None, ALU.mult), ["posf", "cst"], ["ang"])
            for which in range(2):
                if which == 1:
                    sc.op("dve", lambda: nc.vector.tensor_scalar(ang[:], ang[:], 0.5 * math.pi, None, ALU.add), ["ang"], ["ang"])
                sc.op("dve", lambda: nc.vector.tensor_scalar(r1[:], ang[:], 1.0 / TWO_PI, None, ALU.mult), ["ang"], ["r1"])
                sc.op("dve", lambda: nc.vector.tensor_copy(ki[:], r1[:]), ["r1"], ["ki"])
                sc.op("dve", lambda: nc.vector.tensor_copy(r1[:], ki[:]), ["ki"], ["r1"])
                sc.op("dve", lambda: nc.vector.scalar_tensor_tensor(posf[:], r1[:], -CW1, ang[:], ALU.mult, ALU.add), ["r1", "ang"], ["posf"])
                sc.op("dve", lambda: nc.vector.scalar_tensor_tensor(posf[:], r1[:], -CW2, posf[:], ALU.mult, ALU.add), ["r1", "posf"], ["posf"])
                sc.op("dve", lambda: nc.vector.tensor_scalar(posf[:], posf[:], PI_SAFE, -PI_SAFE, ALU.min, ALU.max), ["posf"], ["posf"])
                if which == 0:
                    sc.op("act", lambda: nc.scalar.activation(sinT[:], posf[:], AF.Sin, scale=cst[:, 2 + 2 * col:3 + 2 * col]), ["posf", "cst"], [ksin])
                else:
                    sc.op("act", lambda: nc.scalar.activation(cosT[:], posf[:], AF.Sin), ["posf"], [kcos])

        with ExitStack() as st:
            wr = sb("wr", [128, 8, 2048], BF16, st)
            dtab = sb("dtab", [128, 2, 4, 512], F32, st)
            xitab = sb("xitab", [128, 2, 512], F32, st)
            zeta = sb("zeta", [128, 8], F32, st)
            gng = sb("gng", [128, 512], F32, st)
            cst = sb("cst", [128, 8], F32, st)
            hblk = [sb("hblk%d" % i, [128, 8, 512], BF16, st) for i in range(2)]
            posi = sb("posi", [128, 512], I32, st)
            ki = sb("ki", [128, 512], I32, st)
            posf = sb("posf", [128, 512], F32, st)
            ang = sb("ang", [128, 512], F32, st)
            r1 = sb("r1", [128, 512], F32, st)
            cosR2 = [sb("cosR%d" % i, [128, 512], F32, st) for i in range(2)]
            sinR2 = [sb("sinR%d" % i, [128, 512], F32, st) for i in range(2)]
            t1 = [sb("t1_%d" % i, [128, 512], F32, st) for i in range(2)]
            t2 = [sb("t2_%d" % i, [128, 512], F32, st) for i in range(2)]
            qT = sb("qT", [128, 2, 512], BF16, st)
            qxT = sb("qxT", [128, 2, 512], BF16, st)
            kT = sb("kT", [128, 2, 512], BF16, st)
            ktok = sb("ktok", [128, 4, 2, 128], BF16, st)
            vtok = sb("vtok", [128, 4, 512], BF16, st)
            rgs = sb("rgs", [128, 4, 512], BF16, st)
            PT = sb("PT", [128, 2, 4, 512], BF16, st)
            Sst = sb("Sst", [128, 2, 256], F32, st)
            Sbf = sb("Sbf", [128, 2, 256], BF16, st)
            ybuf = sb("ybuf", [128, 8, 256], F32, st)
            yjunk = sb("yjunk", [128, 256], F32, st)
            ystat = sb("ystat", [128, 8, 8], F32, st)
            mixo = sb("mixo", [128, 4, 512], BF16, st)
            pA = [ps("pA%d" % i, [128, 512], F32, st) for i in range(2)]
            pB = [ps("pB%d" % i, [128, 512], F32, st) for i in range(2)]
            pY = [ps("pY%d" % i, [128, 512], F32, st) for i in range(2)]
            pKV = ps("pKV", [128, 512], F32, st)
            pTR = ps("pTR", [128, 8, 128], BF16, st)

            sc.dma("pool", lambda: nc.gpsimd.dma_start(out=wr[:, :, 0:1024], in_=wmix_d[:, :, 0:1024]), "wr")
            sc.dma("pool", lambda: nc.gpsimd.dma_start(out=wr[:, :, 1024:2048], in_=wmix_d[:, :, 1664:2688]), "wr")
            for j in range(2):
                sc.dma("sp", lambda j=j: nc.sync.dma_start(out=dtab[:, j, :, :], in_=dtab_d[j]), "dtab")
                sc.dma("sp", lambda j=j: nc.sync.dma_start(out=xitab[:, j, :], in_=xitab_d[j]), "xitab")
            sc.dma("sp", lambda: nc.sync.dma_start(out=zeta[:], in_=zeta_d), "zeta")
            sc.dma("sp", lambda: nc.sync.dma_start(out=gng[:], in_=gng_d), "gng")
            sc.dma("sp", lambda: nc.sync.dma_start(out=cst[:], in_=cst_d), "cst")
            sc.op("dve", lambda: nc.vector.memset(Sst[:], 0.0), [], ["Sst"])
            sc.op("dve", lambda: nc.vector.memset(Sbf[:], 0.0), [], ["Sbf"])
            sc.op("dve", lambda: nc.vector.memset(ystat[:], 0.0), [], ["ystat"])
            load_hblk(hblk, 0)
            sbank = [0]
            for tb in range(NB):
                hk = ("hblk", tb % 2)
                hb_ = hblk[tb % 2]
                if tb + 1 < NB:
                    load_hblk(hblk, tb + 1)
                sc.op("dve", lambda: nc.vector.memset(ystat[:, 0:2, :], 0.0), [],
                      ["ystat"] + [("ys1", i) for i in range(8)] + [("ys2", i) for i in range(8)])
                cosR, sinR = cosR2[tb % 2], sinR2[tb % 2]
                kcos, ksin = ("cosR", tb % 2), ("sinR", tb % 2)
                if tb == 0:
                    rope_tables(0, posi, posf, ang, r1, cosR2[0], sinR2[0], cst, 0, ("cosR", 0), ("sinR", 0), ki)
                if KDBG < 1:
                    continue
                for i, (qk, hd) in enumerate([(0, 0), (0, 1), (1, 0), (1, 1)]):
                    b = i % 2
                    c0 = (qk * 4 + hd) * 128
                    c1 = (qk * 4 + 2 + hd) * 128
                    for kc in range(8):
                        sc.op("pe", lambda b=b, kc=kc, c0=c0, hb_=hb_: nc.tensor.matmul(pA[b][:], wr[:, kc, c0:c0 + 128], hb_[:, kc, :],
                                                                                  start=(kc == 0), stop=(kc == 7)), ["wr", hk], [("pA", b)])
                    for kc in range(8):
                        sc.op("pe", lambda b=b, kc=kc, c1=c1, hb_=hb_: nc.tensor.matmul(pB[b][:], wr[:, kc, c1:c1 + 128], hb_[:, kc, :],
                                                                                  start=(kc == 0), stop=(kc == 7)), ["wr", hk], [("pB", b)])
                    sc.op("dve", lambda b=b, cosR=cosR: nc.vector.tensor_tensor(t1[b][:], pA[b][:], cosR[:], ALU.mult), [("pA", b), kcos], [("t1", b)])
                    sc.op("dve", lambda b=b, sinR=sinR: nc.vector.tensor_tensor(t2[b][:], pB[b][:], sinR[:], ALU.mult), [("pB", b), ksin], [("t2", b)])
                    dst = (qT if qk == 0 else kT)
                    dk = ("qT" if qk == 0 else "kT", hd)
                    sc.op("pool", lambda b=b, dst=dst, hd=hd: nc.gpsimd.tensor_tensor(dst[:, hd, :], t1[b][:], t2[b][:], ALU.add),
                          [("t1", b), ("t2", b)], [dk])
                    if qk == 0:
                        sc.op("pool", lambda hd=hd: nc.gpsimd.tensor_tensor(qxT[:, hd, :], qT[:, hd, :], xitab[:, hd, :], ALU.mult),
                              [dk, "xitab"], [("qxT", hd)])
                if tb + 1 < NB:
                    rope_tables(tb + 1, posi, posf, ang, r1, cosR2[(tb + 1) % 2], sinR2[(tb + 1) % 2], cst, 0,
                                ("cosR", (tb + 1) % 2), ("sinR", (tb + 1) % 2), ki)
                if KDBG == 30 and tb == 0:
                    sc.dma("sp", lambda: nc.sync.dma_start(out=out_d[0:128, 0:512], in_=cosR2[0][:]), ("out", 0), [("cosR", 0)], semkey="outst")
                    sc.dma("sp", lambda: nc.sync.dma_start(out=out_d[128:256, 0:512], in_=sinR2[0][:]), ("out", 1), [("sinR", 0)], semkey="outst")
                    sc.dma("pool", lambda: nc.gpsimd.dma_start(out=out_d[256:384, 0:512], in_=qT[:, 0, :]), ("out", 2), [("qT", 0)], semkey="outst")
                    sc.dma("pool", lambda: nc.gpsimd.dma_start(out=out_d[384:512, 0:512], in_=kT[:, 0, :]), ("out", 3), [("kT", 0)], semkey="outst")
                for which in range(2):
                    for tt in range(4):
                        b = (which * 4 + tt) % 2
                        for kc in range(8):
                            sc.op("pe", lambda b=b, kc=kc, tt=tt, which=which, hb_=hb_: nc.tensor.matmul(
                                pA[b][:], hb_[:, kc, tt * 128:(tt + 1) * 128], wr[:, kc, 1024 + which * 512:1536 + which * 512],
                                start=(kc == 0), stop=(kc == 7)), ["wr", hk], [("pA", b)])
                        if which == 0:
                            sc.op("act", lambda b=b, tt=tt: nc.scalar.copy(vtok[:, tt, :], pA[b][:]), [("pA", b)], [("vtok", tt)])
                        else:
                            sc.op("act", lambda b=b, tt=tt: nc.scalar.activation(rgs[:, tt, :], pA[b][:], AF.Silu), [("pA", b)], [("rgs", tt)])
                if KDBG < 2:
                    continue
                for hd in range(2):
                    for tt in range(4):
                        sc.op("pe", lambda hd=hd, tt=tt: nc.tensor.transpose(pTR[:, tt, :], kT[:, hd, tt * 128:(tt + 1) * 128], ident[:]),
                              [("kT", hd), "ident"], ["pTR"])
                    for tt in range(4):
                        sc.op("dve", lambda hd=hd, tt=tt: nc.vector.tensor_scalar(ktok[:, tt, hd, :], pTR[:, tt, :],
                                                                                   zeta[:, hd * 4 + tt:hd * 4 + tt + 1], None, ALU.mult),
                              ["pTR", "zeta"], [("ktok", hd)])
                for hd in range(2):
                    for kt in range(4):
                        N = 512 - kt * 128
                        bi = sbank[0] % 4
                        sbank[0] += 1
                        pt_, pk = (pA[bi], ("pA", bi)) if bi < 2 else (pB[bi - 2], ("pB", bi - 2))
                        sc.op("pe", lambda pt_=pt_, hd=hd, kt=kt, N=N: nc.tensor.matmul(
                            pt_[:, 0:N], kT[:, hd, kt * 128:(kt + 1) * 128], qT[:, hd, kt * 128:512], start=True, stop=True),
                            [("kT", hd), ("qT", hd)], [pk])
                        sc.op("dve", lambda pt_=pt_, hd=hd, kt=kt, N=N: nc.vector.tensor_tensor(
                            PT[:, hd, kt, kt * 128:512], pt_[:, 0:N], dtab[:, hd, kt, kt * 128:512], ALU.mult),
                            [pk, "dtab"], [("PT", hd, kt)])
                if KDBG < 3:
                    continue
                for hd in range(2):
                    for nt in range(4):
                        idx = hd * 4 + nt
                        ysl = idx % 2
                        first = True
                        for kt in range(nt + 1):
                            sc.op("pe", lambda hd=hd, nt=nt, kt=kt, ysl=ysl, first=first: nc.tensor.matmul(
                                pY[ysl][:, 0:256], PT[:, hd, kt, nt * 128:(nt + 1) * 128], vtok[:, kt, hd * 256:(hd + 1) * 256],
                                start=first, stop=False), [("PT", hd, kt), ("vtok", kt)], [("pY", ysl)])
                            first = False
                        sc.op("pe", lambda hd=hd, nt=nt, ysl=ysl: nc.tensor.matmul(
                            pY[ysl][:, 0:256], qxT[:, hd, nt * 128:(nt + 1) * 128], Sbf[:, hd, :], start=False, stop=True),
                            [("qxT", hd), ("Sbf", hd)], [("pY", ysl)])
                        sc.op("act", lambda idx=idx, ysl=ysl: nc.scalar.copy(ybuf[:, idx, :], pY[ysl][:, 0:256]),
                              [("pY", ysl)], [("ybuf", idx)])
                        sc.op("dve", lambda idx=idx: nc.vector.reduce_sum(ystat[:, 0, idx:idx + 1], ybuf[:, idx, :], AX.X),
                              [("ybuf", idx)], [("ys1", idx)])
                        sc.op("act", lambda idx=idx: nc.scalar.activation(yjunk[:], ybuf[:, idx, :], AF.Square,
                                                                         accum_out=ystat[:, 1, idx:idx + 1]),
                              [("ybuf", idx)], ["yjunk", ("ys2", idx)])
                if KDBG == 31 and tb == 0:
                    for nt in range(4):
                        sc.dma("sp", lambda nt=nt: nc.sync.dma_start(out=out_d[nt * 128:(nt + 1) * 128, 0:256], in_=ybuf[:, nt, :]),
                               ("out", nt), [("ybuf", nt)], semkey="outst")
                        sc.dma("pool", lambda nt=nt: nc.gpsimd.dma_start(out=out_d[nt * 128:(nt + 1) * 128, 256:768], in_=vtok[:, nt, :]),
                               ("out", nt), [("vtok", nt)], semkey="outst")
                        sc.dma("pool", lambda nt=nt: nc.gpsimd.dma_start(out=out_d[nt * 128:(nt + 1) * 128, 768:1024], in_=rgs[:, nt, 0:256]),
                               ("out", nt), [("rgs", nt)], semkey="outst")
                for hd in range(2):
                    for tt in range(4):
                        sc.op("pe", lambda hd=hd, tt=tt: nc.tensor.matmul(pKV[:, 0:256], ktok[:, tt, hd, :], vtok[:, tt, hd * 256:(hd + 1) * 256],
                                                                          start=(tt == 0), stop=(tt == 3)),
                              [("ktok", hd), ("vtok", tt)], ["pKV"])
                    sc.op("dve", lambda hd=hd: nc.vector.scalar_tensor_tensor(Sst[:, hd, :], Sst[:, hd, :], xitab[:, hd, 511:512],
                                                                             pKV[:, 0:256], ALU.mult, ALU.add),
                          ["pKV", "Sst", "xitab"], ["Sst"])
                    sc.op("act", lambda hd=hd: nc.scalar.copy(Sbf[:, hd, :], Sst[:, hd, :]), ["Sst"], [("Sbf", hd)])
                if KDBG < 4:
                    continue
                rd = [("ys1", i) for i in range(8)] + [("ys2", i) for i in range(8)]
                S1, S2, MEAN, VAR, RSTD, NBI = [ystat[:, j, :] for j in range(6)]
                ks = "ystat"
                sc.op("dve", lambda: nc.vector.tensor_scalar(MEAN, S1, 1.0 / 256, None, ALU.mult), rd + [ks], [ks])
                sc.op("dve", lambda: nc.vector.tensor_tensor(VAR, MEAN, MEAN, ALU.mult), [ks], [ks])
                sc.op("dve", lambda: nc.vector.scalar_tensor_tensor(VAR, S2, 1.0 / 256, VAR, ALU.mult, ALU.subtract), [ks] + rd, [ks])
                sc.op("dve", lambda: nc.vector.tensor_scalar(VAR, VAR, EPS, None, ALU.add), [ks], [ks])
                sc.op("act", lambda: nc.scalar.sqrt(RSTD, VAR), [ks], [ks])
                sc.op("dve", lambda: nc.vector.reciprocal(RSTD, RSTD), [ks], [ks])
                sc.op("dve", lambda: nc.vector.scalar_tensor_tensor(NBI, MEAN, -1.0, RSTD, ALU.mult, ALU.mult), [ks], [ks])
                for hd in range(2):
                    for nt in range(4):
                        idx = hd * 4 + nt
                        sc.op("act", lambda idx=idx: nc.scalar.activation(ybuf[:, idx, :], ybuf[:, idx, :], AF.Identity,
                                                                         bias=ystat[:, 5, idx:idx + 1], scale=ystat[:, 4, idx:idx + 1]),
                              [("ybuf", idx), ks], [("ybuf", idx)])
                        sc.op("dve", lambda idx=idx, hd=hd: nc.vector.tensor_tensor(ybuf[:, idx, :], ybuf[:, idx, :], gng[:, hd * 256:(hd + 1) * 256], ALU.mult),
                              [("ybuf", idx), "gng"], [("ybuf", idx)])
                        sc.op("pool", lambda idx=idx, hd=hd, nt=nt: nc.gpsimd.tensor_tensor(mixo[:, nt, hd * 256:(hd + 1) * 256], ybuf[:, idx, :],
                                                                                            rgs[:, nt, hd * 256:(hd + 1) * 256], ALU.mult),
                              [("ybuf", idx), ("rgs", nt)], ["mixo"])
                sc.dma("sp", lambda tb=tb: nc.sync.dma_start(
                    out=mix_src[tb][:, 0:512].rearrange("(n p) c -> p n c", p=128), in_=mixo[:]),
                    ("mix_src", tb, 0), ["mixo"], semkey="mix_src0")
            sc.flush()

        if stop_after == "C1":
            return nc
        NKT = S // 128
        with ExitStack() as st:
            wml = sb("wml", [128, 8, 640], BF16, st)
            wq_f = sb("wq_f", [128, 2, 512], F32, st)
            wkv_f = sb("wkv_f", [128, 2, 512], F32, st)
            wq = sb("wq", [128, 2, 512], BF16, st)
            wkv = sb("wkv", [128, 2, 512], BF16, st)
            ng = sb("ng", [128, 4], F32, st)
            cst = sb("cst", [128, 8], F32, st)
            onesf = sb("onesf", [128, 128], BF16, st)
            kn = sb("kn", [128, 2, S], BF16, st)
            kpe = sb("kpe", [64, S], BF16, st)
            vaug = sb("vaug", [128, NKT, 2, 130], BF16, st)
            hblk = [sb("hblk%d" % i, [128, 8, 512], BF16, st) for i in range(2)]
            posi = sb("posi", [128, 512], I32, st)
            ki = sb("ki", [128, 512], I32, st)
            posf = sb("posf", [128, 512], F32, st)
            ang = sb("ang", [128, 512], F32, st)
            r1 = sb("r1", [128, 512], F32, st)
            cosM2 = [sb("cosM%d" % i, [128, 512], F32, st) for i in range(2)]
            sinM2 = [sb("sinM%d" % i, [128, 512], F32, st) for i in range(2)]
            latf = sb("latf", [128, 4, 512], F32, st)
            sq = sb("sq", [128, 4, 512], BF16, st)
            latn = sb("latn", [128, 4, 512], BF16, st)
            rstd = sb("rstd", [128, 2, 512], F32, st)
            t1 = [sb("t1_%d" % i, [64, 512], F32, st) for i in range(2)]
            t2 = [sb("t2_%d" % i, [64, 512], F32, st) for i in range(2)]
            qn = sb("qn", [128, 2, 512], BF16, st)
            qpe = sb("qpe", [64, 2, 512], BF16, st)
            PTa = [sb("PTa%d" % i, [128, 512], BF16, st) for i in range(4)]
            rl = sb("rl", [128, 8], F32, st)
            oout = sb("oout", [128, 4, 256], BF16, st)
            pA = [ps("pA%d" % i, [128, 512], F32, st) for i in range(2)]
            pB = [ps("pB%d" % i, [128, 512], F32, st) for i in range(2)]
            pS = [ps("pS%d" % i, [128, 512], F32, st) for i in range(4)]

            sc.dma("pool", lambda: nc.gpsimd.dma_start(out=wml[:], in_=wmix_d[:, :, 1024:1664]), "wml")
            sc.dma("sp", lambda: nc.sync.dma_start(out=wq_f[:], in_=wuq_d), "wq_f")
            sc.dma("sp", lambda: nc.sync.dma_start(out=wkv_f[:], in_=wukv_d), "wkv_f")
            sc.dma("sp", lambda: nc.sync.dma_start(out=ng[:], in_=ng_d), "ng")
            sc.dma("sp", lambda: nc.sync.dma_start(out=cst[:], in_=cst_d), "cst")
            for c in range(2):
                sc.op("dve", lambda c=c: nc.vector.tensor_scalar(wq[:, c, :], wq_f[:, c, :], ng[:, c:c + 1], None, ALU.mult),
                      ["wq_f", "ng"], ["wq"])
                sc.op("dve", lambda c=c: nc.vector.tensor_scalar(wkv[:, c, :], wkv_f[:, c, :], ng[:, 2 + c:3 + c], None, ALU.mult),
                      ["wkv_f", "ng"], ["wkv"])
            sc.op("dve", lambda: nc.vector.memset(onesf[:], 1.0 / 256.0), [], ["onesf"])
            sc.op("pool", lambda: nc.gpsimd.memset(vaug[:], 1.0), [], ["vaug"] + [("vaug", i) for i in range(NB)])
            load_hblk(hblk, 0)
            cntS = [0]
            cntP = [0]
            ACC = [(pA[0], ("pA", 0)), (pA[1], ("pA", 1)), (pB[0], ("pB", 0)), (pB[1], ("pB", 1))]
            for tb in range(NB):
                hk = ("hblk", tb % 2)
                hb_ = hblk[tb % 2]
                g0 = tb * 512
                if tb + 1 < NB:
                    load_hblk(hblk, tb + 1)
                cosM, sinM = cosM2[tb % 2], sinM2[tb % 2]
                kcm, ksm = ("cosM", tb % 2), ("sinM", tb % 2)
                if tb == 0:
                    rope_tables(0, posi, posf, ang, r1, cosM2[0], sinM2[0], cst, 1, ("cosM", 0), ("sinM", 0), ki)
                if KDBG < 5:
                    continue
                for i in range(4):
                    b = i % 2
                    for kc in range(8):
                        sc.op("pe", lambda b=b, kc=kc, i=i, hb_=hb_: nc.tensor.matmul(pA[b][:], wml[:, kc, i * 128:(i + 1) * 128], hb_[:, kc, :],
                                                                                 start=(kc == 0), stop=(kc == 7)), ["wml", hk], [("pA", b)])
                    sc.op("act", lambda b=b, i=i: nc.scalar.copy(latf[:, i, :], pA[b][:]), [("pA", b)], [("latf", i)])
                    sc.op("act", lambda i=i: nc.scalar.activation(sq[:, i, :], latf[:, i, :], AF.Square), [("latf", i)], [("sq", i)])
                for w in range(2):
                    for c in range(2):
                        sc.op("pe", lambda w=w, c=c: nc.tensor.matmul(pB[w][:], onesf[:], sq[:, w * 2 + c, :], start=(c == 0), stop=(c == 1)),
                              ["onesf", ("sq", w * 2 + c)], [("pB", w)])
                    sc.op("dve", lambda w=w: nc.vector.tensor_scalar(rstd[:, w, :], pB[w][:], EPS, None, ALU.add), [("pB", w)], [("rstd", w)])
                    sc.op("act", lambda w=w: nc.scalar.sqrt(rstd[:, w, :], rstd[:, w, :]), [("rstd", w)], [("rstd", w)])
                    sc.op("dve", lambda w=w: nc.vector.reciprocal(rstd[:, w, :], rstd[:, w, :]), [("rstd", w)], [("rstd", w)])
                    if w == 0:
                        sc.op("dve", lambda: nc.vector.tensor_scalar(rstd[:, 0, :], rstd[:, 0, :], ATT_SCALE, None, ALU.mult), [("rstd", 0)], [("rstd", 0)])
                    for c in range(2):
                        i = w * 2 + c
                        if c == 0:
                            sc.op("dve", lambda i=i, w=w: nc.vector.tensor_tensor(latn[:, i, :], latf[:, i, :], rstd[:, w, :], ALU.mult),
                                  [("latf", i), ("rstd", w)], [("latn", i)])
                        else:
                            sc.op("pool", lambda i=i, w=w: nc.gpsimd.tensor_tensor(latn[:, i, :], latf[:, i, :], rstd[:, w, :], ALU.mult),
                                  [("latf", i), ("rstd", w)], [("latn", i)])

                def rope64(psA, psB, kA, kB, dst_ap, dkey, slot):
                    sc.op("dve", lambda cosM=cosM: nc.vector.tensor_tensor(t1[slot][:], psA[0:64, :], cosM[0:64, :], ALU.mult), [kA, kcm], [("t1", slot)])
                    sc.op("dve", lambda sinM=sinM: nc.vector.tensor_tensor(t2[slot][:], psB[0:64, :], sinM[0:64, :], ALU.mult), [kB, ksm], [("t2", slot)])
                    sc.op("pool", lambda: nc.gpsimd.tensor_tensor(dst_ap, t1[slot][:], t2[slot][:], ALU.add), [("t1", slot), ("t2", slot)], [dkey])

                if KDBG < 6:
                    continue
                for kc in range(8):
                    sc.op("pe", lambda kc=kc, hb_=hb_: nc.tensor.matmul(pA[0][0:64, :], wml[:, kc, 512:576], hb_[:, kc, :], start=(kc == 0), stop=(kc == 7)),
                          ["wml", hk], [("pA", 0)])
                for kc in range(8):
                    sc.op("pe", lambda kc=kc, hb_=hb_: nc.tensor.matmul(pB[0][0:64, :], wml[:, kc, 576:640], hb_[:, kc, :], start=(kc == 0), stop=(kc == 7)),
                          ["wml", hk], [("pB", 0)])
                rope64(pA[0], pB[0], ("pA", 0), ("pB", 0), kpe[:, g0:g0 + 512], ("kpe", tb), 0)
                for hd in range(2):
                    for c in range(2):
                        sc.op("pe", lambda hd=hd, c=c: nc.tensor.matmul(pA[1][:], wq[:, c, hd * 256:hd * 256 + 128], latn[:, c, :], start=(c == 0), stop=(c == 1)),
                              ["wq", ("latn", c)], [("pA", 1)])
                    sc.op("act", lambda hd=hd: nc.scalar.copy(qn[:, hd, :], pA[1][:]), [("pA", 1)], [("qn", hd)])
                    for c in range(2):
                        sc.op("pe", lambda hd=hd, c=c: nc.tensor.matmul(pA[0][0:64, :], wq[:, c, hd * 256 + 128:hd * 256 + 192], latn[:, c, :], start=(c == 0), stop=(c == 1)),
                              ["wq", ("latn", c)], [("pA", 0)])
                    for c in range(2):
                        sc.op("pe", lambda hd=hd, c=c: nc.tensor.matmul(pB[0][0:64, :], wq[:, c, hd * 256 + 192:hd * 256 + 256], latn[:, c, :], start=(c == 0), stop=(c == 1)),
                              ["wq", ("latn", c)], [("pB", 0)])
                    rope64(pA[0], pB[0], ("pA", 0), ("pB", 0), qpe[:, hd, :], ("qpe", hd), 1)
                    for c in range(2):
                        sc.op("pe", lambda hd=hd, c=c: nc.tensor.matmul(pB[1][:], wkv[:, c, hd * 128:(hd + 1) * 128], latn[:, 2 + c, :], start=(c == 0), stop=(c == 1)),
                              ["wkv", ("latn", 2 + c)], [("pB", 1)])
                    sc.op("act", lambda hd=hd, g0=g0: nc.scalar.copy(kn[:, hd, g0:g0 + 512], pB[1][:]), [("pB", 1)], [("kn", tb)])
                for tt in range(4):
                    b = tt % 2
                    for c in range(2):
                        sc.op("pe", lambda b=b, tt=tt, c=c: nc.tensor.matmul(pA[b][:, 0:256], latn[:, 2 + c, tt * 128:(tt + 1) * 128], wkv[:, c, 256:512],
                                                                            start=(c == 0), stop=(c == 1)), ["wkv", ("latn", 2 + c)], [("pA", b)])
                    sc.op("act", lambda b=b, tt=tt, tb=tb: nc.scalar.copy(vaug[:, tb * 4 + tt, :, 0:128], pA[b][:, 0:256].rearrange("p (h d) -> p h d", h=2)),
                          [("pA", b)], [("vaug", tb)])
                if KDBG == 41 and tb == 0:
                    sc.dma("pool", lambda: nc.gpsimd.dma_start(out=out_d[0:128, 0:512], in_=latn[:, 2, :]), ("out", 0), [("latn", 2)], semkey="outst")
                    sc.dma("pool", lambda: nc.gpsimd.dma_start(out=out_d[128:256, 0:512], in_=wkv[:, 0, :]), ("out", 1), ["wkv"], semkey="outst")
                    sc.dma("sp", lambda: nc.sync.dma_start(out=out_d[256:384, 0:512], in_=rstd[:, 1, :]), ("out", 2), [("rstd", 1)], semkey="outst")
                    sc.dma("sp", lambda: nc.sync.dma_start(out=out_d[384:512, 0:512], in_=latf[:, 2, :]), ("out", 3), [("latf", 2)], semkey="outst")
                if KDBG == 40 and tb == 0:
                    sc.dma("pool", lambda: nc.gpsimd.dma_start(out=out_d[0:128, 0:512], in_=qn[:, 0, :]), ("out", 0), [("qn", 0)], semkey="outst")
                    sc.dma("pool", lambda: nc.gpsimd.dma_start(out=out_d[128:256, 0:512], in_=kn[:, 0, 0:512]), ("out", 1), [("kn", 0)], semkey="outst")
                    sc.dma("pool", lambda: nc.gpsimd.dma_start(out=out_d[256:320, 0:512], in_=qpe[:, 0, :]), ("out", 2), [("qpe", 0)], semkey="outst")
                    sc.dma("pool", lambda: nc.gpsimd.dma_start(out=out_d[320:384, 0:512], in_=kpe[:, 0:512]), ("out", 3), [("kpe", 0)], semkey="outst")
                    sc.dma("pool", lambda: nc.gpsimd.dma_start(out=out_d[384:512, 0:130], in_=vaug[:, 0, 0, :]), ("out", 4), [("vaug", 0)], semkey="outst")
                    sc.dma("sp", lambda: nc.sync.dma_start(out=out_d[384:512, 512:1024], in_=rstd[:, 0, :]), ("out", 5), [("rstd", 0)], semkey="outst")
                if KDBG < 7:
                    continue
                if tb + 1 < NB:
                    rope_tables(tb + 1, posi, posf, ang, r1, cosM2[(tb + 1) % 2], sinM2[(tb + 1) % 2], cst, 1,
                                ("cosM", (tb + 1) % 2), ("sinM", (tb + 1) % 2), ki)
                nkt = 4 * tb + 4
                tiles = [(hd, kt) for hd in range(2) for kt in range(nkt)]
                LA = 3

                def emit_scores(hd, kt, idx):
                    j = max(0, kt - 4 * tb)
                    qoff = j * 128
                    N = 512 - qoff
                    sbk = idx % 4
                    kb = kt // 4
                    sc.op("pe", lambda: nc.tensor.matmul(
                        pS[sbk][:, 0:N], kn[:, hd, kt * 128:(kt + 1) * 128], qn[:, hd, qoff:512], start=True, stop=False),
                        [("kn", kb), ("qn", hd)], [("pS", sbk)])
                    sc.op("pe", lambda: nc.tensor.matmul(
                        pS[sbk][:, 0:N], kpe[:, kt * 128:(kt + 1) * 128], qpe[:, hd, qoff:512], start=False, stop=True),
                        [("kpe", kb), ("qpe", hd)], [("pS", sbk)])
                    sc.op("act", lambda: nc.scalar.activation(PTa[sbk][:, 0:N], pS[sbk][:, 0:N], AF.Exp),
                          [("pS", sbk)], [("PTa", sbk)])
                    if kt >= 4 * tb:
                        sc.op("dve", lambda: nc.vector.memset(PTa[sbk][64:128, 0:64], 0.0), [], [("PTa", sbk)])

                def emit_pv(hd, kt, idx):
                    j = max(0, kt - 4 * tb)
                    qoff = j * 128
                    sbk = idx % 4
                    kb = kt // 4
                    for nt in range(j, 4):
                        acc, akey = ACC[nt]
                        sc.op("pe", lambda nt=nt, acc=acc: nc.tensor.matmul(
                            acc[:, 0:130], PTa[sbk][:, nt * 128 - qoff:nt * 128 - qoff + 128], vaug[:, kt, hd, 0:130],
                            start=(kt == 0), stop=(kt == 4 * tb + nt)),
                            [("PTa", sbk), ("vaug", kb)], [akey])
                    if kt == nkt - 1:
                        for nt in range(4):
                            acc, akey = ACC[nt]
                            sc.op("dve", lambda nt=nt, acc=acc: nc.vector.reciprocal(rl[:, hd * 4 + nt:hd * 4 + nt + 1], acc[:, 128:129]),
                                  [akey], [("rl", hd * 4 + nt)])
                            sc.op("dve", lambda nt=nt, acc=acc: nc.vector.tensor_scalar(oout[:, nt, hd * 128:(hd + 1) * 128], acc[:, 0:128],
                                                                                      rl[:, hd * 4 + nt:hd * 4 + nt + 1], None, ALU.mult),
                                  [akey, ("rl", hd * 4 + nt)], ["oout"])

                for i in range(len(tiles) + LA):
                    if i < len(tiles):
                        emit_scores(tiles[i][0], tiles[i][1], cntS[0] + i)
                    if i >= LA:
                        emit_pv(tiles[i - LA][0], tiles[i - LA][1], cntS[0] + i - LA)
                cntS[0] += len(tiles)
                sc.dma("sp", lambda tb=tb: nc.sync.dma_start(
                    out=mix_src[tb][:, 512:768].rearrange("(n p) c -> p n c", p=128), in_=oout[:]),
                    ("mix_src", tb, 1), ["oout"], semkey="mix_src1")
                sc.dma("pool", lambda tb=tb: nc.gpsimd.collective_compute("AllGather", ALU.bypass, replica_groups=RG,
                                                                          ins=[mix_src[tb].opt()], outs=[mix_all[tb].opt()]),
                       ("mix_all", tb), [("mix_src", tb, 0), ("mix_src", tb, 1)], inc=1, semkey="mix_all")
                sc.dma("sp", lambda tb=tb: nc.sync.dma_start(out=mix_comb[tb * 2048:(tb + 1) * 2048, :], in_=mix_all[tb]),
                       "mix_comb", [("mix_all", tb)])
            sc.flush()

        if stop_after == "C2":
            return nc
        with ExitStack() as st:
            mixTs = [sb("mixTs%d" % i, [128, 512], BF16, st) for i in range(2)]
            wro = sb("wro", [128, 16, D], BF16, st)
            wmo = sb("wmo", [128, 8, D], BF16, st)
            wg = sb("wg", [128, 8, 2048], BF16, st)
            mixin = [sb("mixin%d" % i, [128, 4, 768], BF16, st) for i in range(2)]
            ygT = sb("ygT", [128, 24, 512], BF16, st)
            h1T = [sb("h1T%d" % i, [128, 8, 512], BF16, st) for i in range(2)]
            sg = [sb("sg%d" % i, [128, 512], F32, st) for i in range(4)]
            tpp = [ps("tpp%d" % i, [128, 8, 128], BF16, st) for i in range(2)]
            pR = ps("pR", [128, 512], F32, st)
            pM = ps("pM", [128, 512], F32, st)
            pG1 = ps("pG1", [128, 512], F32, st)
            pG2 = ps("pG2", [128, 512], F32, st)
            sc.dma("pool", lambda: nc.gpsimd.dma_start(out=wro[:], in_=wro_d), "wro")
            sc.dma("pool", lambda: nc.gpsimd.dma_start(out=wmo[:], in_=wmo_d), "wmo")
            sc.dma("pool", lambda: nc.gpsimd.dma_start(out=wg[:], in_=wgate_d), "wg")
            def gather_local(j):
                rank = nc.gpsimd.partition_id() % 4
                return nc.gpsimd.dma_start(out=mix_loc[:, j * 512:(j + 1) * 512, :],
                                           in_=mix_comb[bass.ds((rank * NG + j) * 2048, 2048), :].rearrange("(r p) c -> r p c", r=4))

            for j in range(NG):
                sc.dma("pool", lambda j=j: gather_local(j), "mix_loc", ["mix_comb"])
            cnt = [0]
            mcnt = [0]
            for g in range(NG):
                sc.dma("sp", lambda g=g: nc.sync.dma_start(
                    out=h1T[g % 2][:], in_=h1T_src[g].rearrange("(kc p) t -> p kc t", p=128)), ("h1T", g % 2))
                for tt in range(4):
                    t = g * 4 + tt
                    mb = cnt[0] % 2
                    cnt[0] += 1
                    sc.dma("sp", lambda t=t, mb=mb: nc.sync.dma_start(
                        out=mixin[mb][:], in_=mix_loc[:, t * 128:(t + 1) * 128, :].rearrange("r p c -> p r c")), ("mixin", mb), ["mix_loc"])
                    if KDBG in (21, 22, 23, 24):
                        sc.dma("pool", lambda t=t, mb=mb: nc.gpsimd.dma_start(out=out_d[t * 128:(t + 1) * 128, 0:768], in_=mixin[mb][:, KDBG - 21, :]),
                               ("out", t), [("mixin", mb)], semkey="outst")
                    for rr in range(4):
                        for grp in range(2):
                            tb_ = (rr * 2 + grp) % 2
                            ncol = 4 if grp == 0 else 2
                            for jj in range(ncol):
                                col = (jj if grp == 0 else 4 + jj) * 128
                                sc.op("pe", lambda mb=mb, rr=rr, tb_=tb_, jj=jj, col=col: nc.tensor.transpose(
                                    tpp[tb_][:, jj, :], mixin[mb][:, rr, col:col + 128], ident[:]), [("mixin", mb), "ident"], [("tpp", tb_)])
                            ch0 = rr * 4 if grp == 0 else 16 + rr * 2
                            sc.op("dve" if grp == 0 else "act",
                                  (lambda tb_=tb_, ch0=ch0, ncol=ncol, tt=tt: nc.vector.tensor_copy(ygT[:, ch0:ch0 + ncol, tt * 128:(tt + 1) * 128], tpp[tb_][:, 0:ncol, :]))
                                  if grp == 0 else
                                  (lambda tb_=tb_, ch0=ch0, ncol=ncol, tt=tt: nc.scalar.copy(ygT[:, ch0:ch0 + ncol, tt * 128:(tt + 1) * 128], tpp[tb_][:, 0:ncol, :])),
                                  [("tpp", tb_)], [("ygT", tt)])
                ygk = [("ygT", tt) for tt in range(4)]
                hk = ("h1T", g % 2)
                hT_ = h1T[g % 2]
                for dc in range(8):
                    for kc in range(8):
                        sc.op("pe", lambda dc=dc, kc=kc, hT_=hT_: nc.tensor.matmul(pG1[:], wg[:, kc, dc * 128:(dc + 1) * 128], hT_[:, kc, :], start=(kc == 0), stop=(kc == 7)),
                              ["wg", hk], ["pG1"])
                    for kc in range(8):
                        sc.op("pe", lambda dc=dc, kc=kc, hT_=hT_: nc.tensor.matmul(pG2[:], wg[:, kc, 1024 + dc * 128:1024 + (dc + 1) * 128], hT_[:, kc, :], start=(kc == 0), stop=(kc == 7)),
                              ["wg", hk], ["pG2"])
                    for ch in range(16):
                        sc.op("pe", lambda dc=dc, ch=ch: nc.tensor.matmul(pR[:], wro[:, ch, dc * 128:(dc + 1) * 128], ygT[:, ch, :], start=(ch == 0), stop=(ch == 15)),
                              ["wro"] + ygk, ["pR"])
                    for ch in range(8):
                        sc.op("pe", lambda dc=dc, ch=ch: nc.tensor.matmul(pM[:], wmo[:, ch, dc * 128:(dc + 1) * 128], ygT[:, 16 + ch, :], start=(ch == 0), stop=(ch == 7)),
                              ["wmo"] + ygk, ["pM"])
                    sc.op("act", lambda: nc.scalar.activation(sg[0][:], pG1[:], AF.Sigmoid), ["pG1"], [("sg", 0)])
                    sc.op("act", lambda: nc.scalar.activation(sg[1][:], pG2[:], AF.Sigmoid), ["pG2"], [("sg", 1)])
                    sc.op("dve", lambda: nc.vector.tensor_tensor(sg[2][:], sg[0][:], pR[:], ALU.mult), [("sg", 0), "pR"], [("sg", 2)])
                    sc.op("dve", lambda: nc.vector.tensor_tensor(sg[3][:], sg[1][:], pM[:], ALU.mult), [("sg", 1), "pM"], [("sg", 3)])
                    ms = mcnt[0] % 2
                    mcnt[0] += 1
                    sc.op("pool", lambda ms=ms: nc.gpsimd.tensor_tensor(mixTs[ms][:], sg[2][:], sg[3][:], ALU.add),
                          [("sg", 2), ("sg", 3)], [("mixTs", ms)])
                    sc.dma("sp", lambda ms=ms, dc=dc, g=g: nc.sync.dma_start(out=mixT_d[dc * 128:(dc + 1) * 128, g * 512:(g + 1) * 512], in_=mixTs[ms][:]),
                           "mixT_d", [("mixTs", ms)])
            sc.flush()

        hst = ExitStack()
        H["h"] = h = sb("h", [128, NT, D], F32, hst)
        with ExitStack() as st:
            mixT = sb("mixT", [128, 8, T], BF16, st)
            for dc in range(8):
                sc.dma("sp", lambda dc=dc: nc.sync.dma_start(out=mixT[:, dc, :], in_=mixT_d[dc * 128:(dc + 1) * 128, :]), "mixT", ["mixT_d"])
            wo = sb("wo", [128, 8, D], BF16, st)
            lng = sb("lng", [128, D], F32, st)
            lnb = sb("lnb", [128, D], F32, st)
            junk = sb("junk", [128, D], F32, st)
            stat = sb("stat", [128, 8, NT], F32, st)
            po = [ps("po%d" % i, [128, 512], F32, st) for i in range(2)]
            sc.dma("pool", lambda: nc.gpsimd.dma_start(out=wo[:], in_=wout_d), "wo")
            sc.dma("sp", lambda: nc.sync.dma_start(out=lng[:], in_=lnp_d[1]), "lng")
            sc.dma("sp", lambda: nc.sync.dma_start(out=lnb[:], in_=lnp_d[5]), "lnb")
            for t in range(NT):
                sc.dma("sp", lambda t=t: nc.sync.dma_start(out=h[:, t, :], in_=h_spill[t * 128:(t + 1) * 128, :]), ("h", t), semkey=("hload", t // 4))
            c2 = 0
            for t in range(NT):
                for half in range(2):
                    b = c2 % 2
                    c2 += 1
                    for dc in range(8):
                        sc.op("pe", lambda b=b, dc=dc, t=t, half=half: nc.tensor.matmul(
                            po[b][:], mixT[:, dc, t * 128:(t + 1) * 128], wo[:, dc, half * 512:(half + 1) * 512], start=(dc == 0), stop=(dc == 7)),
                            ["wo", "mixT"], [("po", b)])
                    hs = h[:, t, half * 512:(half + 1) * 512]
                    if KDBG == 20:
                        sc.op("dve", lambda b=b, hs=hs: nc.vector.tensor_copy(hs, po[b][:]), [("po", b), ("h", t)], [("h", t)])
                    else:
                        sc.op("dve", lambda b=b, hs=hs: nc.vector.scalar_tensor_tensor(hs, hs, ALPHA, po[b][:], ALU.mult, ALU.add),
                              [("po", b), ("h", t)], [("h", t)])
            if KDBG != 20:
                layer_norm(st, range(NT), 1, lng, lnb, junk, stat)
            sc.flush()

        if stop_after == "D":
            dump_h()
            hst.close()
            return nc

        ffn_phase(1, 2)

        with ExitStack() as st:
            xT = sb("xT", [128, 8, T], BF16, st)
            hb = [sb("hb%d" % i, [128, D], BF16, st) for i in range(2)]
            wpg = sb("wpg", [128, 8, D], BF16, st)
            wpp = sb("wpp", [128, 2, D], BF16, st)
            pb = sb("pb", [128, NT, 256], BF16, st)
            pT = sb("pT", [128, 2, T], BF16, st)
            sgp = [sb("sgp%d" % i, [128, 512], F32, st) for i in range(2)]
            lng = sb("lng", [128, D], F32, st)
            lnb = sb("lnb", [128, D], F32, st)
            junk = sb("junk", [128, D], F32, st)
            stat = sb("stat", [128, 8, NT], F32, st)
            tp = [ps("tp%d" % i, [128, 8, 128], BF16, st) for i in range(2)]
            pG = [ps("pG%d" % i, [128, 512], F32, st) for i in range(2)]
            pP = [ps("pP%d" % i, [128, 512], F32, st) for i in range(2)]
            sc.dma("pool", lambda: nc.gpsimd.dma_start(out=wpg[:], in_=wpg_d), "wpg")
            sc.dma("pool", lambda: nc.gpsimd.dma_start(out=wpp[:], in_=wpp_d), "wpp")
            sc.dma("pool", lambda: nc.gpsimd.dma_start(out=pb[:], in_=p_d.rearrange("(n p) c -> p n c", p=128)), "pb")
            sc.dma("sp", lambda: nc.sync.dma_start(out=lng[:], in_=lnp_d[3]), "lng")
            sc.dma("sp", lambda: nc.sync.dma_start(out=lnb[:], in_=lnp_d[7]), "lnb")
            make_xT(st, xT, hb, tp, True)
            for t in range(NT):
                b = t % 2
                for c in range(2):
                    sc.op("pe", lambda b=b, t=t, c=c: nc.tensor.transpose(tp[b][:, c, :], pb[:, t, c * 128:(c + 1) * 128], ident[:]),
                          ["pb", "ident"], [("tp", b)])
                sc.op("dve", lambda b=b, t=t: nc.vector.tensor_copy(pT[:, :, t * 128:(t + 1) * 128], tp[b][:, 0:2, :]), [("tp", b)], [("pT", t)])
            c3 = 0
            for t in range(NT):
                for half in range(2):
                    b = c3 % 2
                    c3 += 1
                    for kc in range(8):
                        sc.op("pe", lambda b=b, kc=kc, t=t, half=half: nc.tensor.matmul(
                            pG[b][:], xT[:, kc, t * 128:(t + 1) * 128], wpg[:, kc, half * 512:(half + 1) * 512], start=(kc == 0), stop=(kc == 7)),
                            ["wpg", ("xT", t // 4)], [("pG", b)])
                    for c in range(2):
                        sc.op("pe", lambda b=b, c=c, t=t, half=half: nc.tensor.matmul(
                            pP[b][:], pT[:, c, t * 128:(t + 1) * 128], wpp[:, c, half * 512:(half + 1) * 512], start=(c == 0), stop=(c == 1)),
                            ["wpp", ("pT", t)], [("pP", b)])
                    sc.op("act", lambda b=b: nc.scalar.activation(sgp[b][:], pG[b][:], AF.Sigmoid), [("pG", b)], [("sgp", b)])
                    sc.op("dve", lambda b=b: nc.vector.tensor_tensor(sgp[b][:], sgp[b][:], pP[b][:], ALU.mult), [("sgp", b), ("pP", b)], [("sgp", b)])
                    hs = h[:, t, half * 512:(half + 1) * 512]
                    sc.op("pool", lambda b=b, hs=hs: nc.gpsimd.tensor_tensor(hs, hs, sgp[b][:], ALU.add), [("sgp", b), ("h", t)], [("h", t)])

            def store(t):
                sc.dma("sp", lambda t=t: nc.sync.dma_start(out=out_d[t * 128:(t + 1) * 128, :], in_=h[:, t, :]), ("out", t), [("h", t)], semkey="outst")

            layer_norm(st, range(NT), 3, lng, lnb, junk, stat, store)
            sc.flush()
        hst.close()
    return nc


def _kmajor(w):
    K, N = w.shape
    return np.ascontiguousarray(w.reshape(K // 128, 128, N).transpose(1, 0, 2))


def prep_inputs(inp, S):
    T = S // 4
    f = np.float32
    ln_g = np.asarray(inp["ln_g"], f)[0]
    ln_b = np.asarray(inp["ln_b"], f)[0]
    lnp = np.ascontiguousarray(np.broadcast_to(np.concatenate([ln_g, ln_b], 0)[:, None, :], (8, 128, D)))

    def w1_layout(w):
        w = np.asarray(w, f)[0]
        perm = []
        for c0, c1 in PIECES:
            perm += list(range(c0 * 128, c1 * 128)) + list(range(DFF + c0 * 128, DFF + c1 * 128))
        return _kmajor(w[:, perm])

    w_in = np.asarray(inp["w_in"], f)[0]
    w_uq = np.asarray(inp["w_uq"], f)[0]
    w_ukv = np.asarray(inp["w_ukv"], f)[0]
    gn = np.asarray(inp["ret_gn_g"], f)[0]
    cst = np.zeros((128, 8), f)
    d = np.arange(128)
    cst[:, 0] = (10000.0 ** (-(d % 64).astype(np.float64) / 64.0)).astype(f)
    cst[:, 1] = (10000.0 ** (-(d % 32).astype(np.float64) / 32.0)).astype(f)
    sgn_r = np.where(d < 64, -1.0, 1.0)
    sgn_m = np.where((d % 64) < 32, -1.0, 1.0)
    cst[:, 2] = sgn_r
    cst[:, 3] = sgn_r * PI_SAFE
    cst[:, 4] = sgn_m
    cst[:, 5] = sgn_m * PI_SAFE
    cst[:, 6] = PI_SAFE
    common = {
        "lnp": lnp,
        "ffn1_w1": w1_layout(inp["ffn1_w_in"]),
        "ffn2_w1": w1_layout(inp["ffn2_w_in"]),
        "ffn1_w2": _kmajor(np.asarray(inp["ffn1_w_out"], f)[0]),
        "ffn2_w2": _kmajor(np.asarray(inp["ffn2_w_out"], f)[0]),
        "ident": np.eye(128, dtype=f),
        "wgate": _kmajor(w_in[:, 6720:8768]),
        "ng": np.ascontiguousarray(np.concatenate([np.asarray(inp["q_norm_g"], f)[0].reshape(2, 128).T,
                                                   np.asarray(inp["kv_norm_g"], f)[0].reshape(2, 128).T], 1)),
        "wro": _kmajor(np.asarray(inp["w_ret_o"], f)[0]),
        "wmo": _kmajor(np.asarray(inp["w_mla_o"], f)[0]),
        "wout": _kmajor(np.asarray(inp["w_out"], f)[0]),
        "wpg": _kmajor(np.asarray(inp["ple_w_gate"], f)[0]),
        "wpp": _kmajor(np.asarray(inp["ple_w_proj"], f)[0]),
        "cst": cst,
    }
    x = np.asarray(inp["x"], f)
    p = np.asarray(inp["p"], f)[0]
    pos = np.asarray(inp["positions"], np.int32)
    rot128 = np.concatenate([np.arange(64, 128), np.arange(0, 64)])
    rot64 = np.concatenate([np.arange(32, 64), np.arange(0, 32)])
    n = np.arange(512)
    maps = []
    for core in range(8):
        b, r = core // 4, core % 4
        heads = (2 * r, 2 * r + 1)
        m = dict(common)
        m["x"] = np.ascontiguousarray(x[b, r * T:(r + 1) * T])
        m["p"] = np.ascontiguousarray(p[b, r * T:(r + 1) * T])
        m["pos"] = np.ascontiguousarray(np.broadcast_to(pos[b][None, :], (128, S)))
        cols = []
        for base in (0, 1024):
            for hh in heads:
                cols.append(base + hh * 128 + np.arange(128))
            for hh in heads:
                cols.append(base + hh * 128 + rot128)
        cols.append(6144 + np.arange(256))
        cols.append(6400 + np.arange(256))
        cols.append(6656 + np.arange(64))
        cols.append(6656 + rot64)
        for base in (2048, 4096):
            for hh in heads:
                cols.append(base + hh * 256 + np.arange(256))
        cols = np.concatenate(cols)
        assert cols.shape[0] == MIXW
        m["wmix"] = _kmajor(w_in[:, cols])
        cq = []
        for hh in heads:
            cq.append(hh * 192 + np.arange(128))
            cq.append(hh * 192 + 128 + np.arange(64))
            cq.append(hh * 192 + 128 + rot64)
        m["wuq"] = _kmajor(w_uq[:, np.concatenate(cq)])
        ckv = [hh * 256 + np.arange(128) for hh in heads] + [hh * 256 + 128 + np.arange(128) for hh in heads]
        m["wukv"] = _kmajor(w_ukv[:, np.concatenate(ckv)])
        m["gng"] = np.ascontiguousarray(np.broadcast_to(
            np.concatenate([gn[hh * 256:(hh + 1) * 256] for hh in heads])[None, :], (128, 512)))
        dtab = np.zeros((2, 128, 4, 512), f)
        xit = np.zeros((2, 128, 512), f)
        zeta = np.zeros((128, 8), f)
        for j, hh in enumerate(heads):
            lg = math.log(1.0 - 2.0 ** (-5.0 - hh))
            for kt in range(4):
                mm = kt * 128 + np.arange(128)
                dist = np.abs(n[None, :] - mm[:, None]).astype(np.float64)
                ok = (mm[:, None] // 64) <= (n[None, :] // 64)
                dtab[j, :, kt, :] = np.where(ok, np.exp(lg * dist) * (128.0 ** -0.5), 0.0)
                zeta[:, j * 4 + kt] = np.exp(lg * (511.0 - mm)) * (128.0 ** -0.5)
            xit[j, :, :] = np.exp(lg * (n + 1.0))[None, :]
        m["dtab"] = dtab
        m["xitab"] = xit
        m["zeta"] = zeta
        maps.append(m)
    return maps


_PROG = {}


def run(inputs, S, stop_after=None):
    key = (S, stop_after)
    if key not in _PROG:
        _PROG[key] = build_program(S, stop_after)
    nc = _PROG[key]
    maps = prep_inputs(inputs, S)
    res = run_bass_kernel_spmd(nc, maps, core_ids=list(range(8)))
    T = S // 4
    out = np.zeros((2, S, D), np.float32)
    for core in range(8):
        b, r = core // 4, core % 4
        out[b, r * T:(r + 1) * T] = np.asarray(res.results[core]["out"])
    return out


def kernel(**inputs):
    return run(inputs, 8192)
```
